# Optimizing a Trainium2 kernel written in Bass

```python
import jax, jax.numpy as jnp
from jax import lax
import numpy as np

D_MODEL = 1024
BATCH = 8
SEQ = 4096
DEPTH = 2

A_WIDTH = D_MODEL // 2
A_GROUPS = 4
A_GROUP_DIM = A_WIDTH // A_GROUPS
CHUNK = 128
B_WIDTH = D_MODEL // 2
B_HEAD_DIM = 64
B_HEADS = B_WIDTH // B_HEAD_DIM
DILATED_PATTERNS = ((128, 1), (512, 4), (2048, 16))
Q_BLOCK = 128
ROPE_THETA = 10000.0
AB_IN = 2 * A_WIDTH + 3 * B_WIDTH
CONV_WIDTH = 31
FFN_DIM = 2816
FFN_CONV_WIDTH = 3
EPS = 1e-6
NEG = -1e30
N_EVEN = (DEPTH + 1) // 2
N_ODD = DEPTH // 2

kernel_name = "hybrid_gmlp_dilated_conformer_convffn"


def rms_norm(x, g):
    xf = x.astype(jnp.float32)
    y = xf * lax.rsqrt(jnp.mean(xf * xf, axis=-1, keepdims=True) + EPS)
    return (y * g.astype(jnp.float32)).astype(x.dtype)


def layer_norm(x, g):
    xf = x.astype(jnp.float32)
    mu = jnp.mean(xf, axis=-1, keepdims=True)
    var = jnp.mean(jnp.square(xf - mu), axis=-1, keepdims=True)
    return ((xf - mu) * lax.rsqrt(var + EPS) * g.astype(jnp.float32)).astype(x.dtype)


def causal_depthwise_conv(x, w, b):
    k = w.shape[0]
    y = lax.conv_general_dilated(
        x, w[:, None, :].astype(x.dtype), window_strides=(1,), padding=((k - 1, 0),),
        dimension_numbers=("NWC", "WIO", "NWC"), feature_group_count=x.shape[-1])
    return y + b


def rotary(x, positions):
    e = x.shape[-1]
    inv_freq = 1.0 / (ROPE_THETA ** (jnp.arange(0, e, 2, dtype=jnp.float32) / e))
    ang = positions.astype(jnp.float32)[..., None] * inv_freq
    cos, sin = jnp.cos(ang)[:, :, None, :], jnp.sin(ang)[:, :, None, :]
    xf = x.astype(jnp.float32)
    x1, x2 = xf[..., : e // 2], xf[..., e // 2:]
    return jnp.concatenate([x1 * cos - x2 * sin, x2 * cos + x1 * sin], -1).astype(x.dtype)


def dilated_branch(q, k, v, window, dilation):
    bn, s, h, e = q.shape
    L = s // dilation
    w = window // dilation
    qb_len = min(Q_BLOCK, L)
    nb = L // qb_len

    def to_sub(t):
        return t.reshape(bn, L, dilation, h, e).transpose(0, 2, 3, 1, 4)

    qs, ks, vs = to_sub(q), to_sub(k), to_sub(v)
    pad = ((0, 0), (0, 0), (0, 0), (w, 0), (0, 0))
    kp, vp = jnp.pad(ks, pad), jnp.pad(vs, pad)
    starts = jnp.arange(nb) * qb_len
    j = jnp.arange(w + qb_len)
    idx = starts[:, None] + j[None, :]
    kb = jnp.take(kp, idx, axis=3)
    vb = jnp.take(vp, idx, axis=3)
    qb = qs.reshape(bn, dilation, h, nb, qb_len, e)
    sc = jnp.einsum("brhnqe,brhnke->brhnqk", qb, kb).astype(jnp.float32) * (e ** -0.5)
    i = jnp.arange(qb_len)
    dist = i[:, None] + w - j[None, :]
    kpos = starts[:, None, None] + j[None, None, :] - w
    mask = (dist >= 0)[None] & (dist <= w)[None] & (kpos >= 0)
    sc = jnp.where(mask, sc, NEG)
    m = jnp.max(sc, axis=-1, keepdims=True)
    p = jnp.exp(sc - m)
    den = jnp.sum(p, axis=-1, keepdims=True)
    o = jnp.einsum("brhnqk,brhnke->brhnqe", p, vb.astype(jnp.float32)) / den
    lse = (m + jnp.log(den))[..., 0]
    o = o.reshape(bn, dilation, h, L, e).transpose(0, 3, 1, 2, 4).reshape(bn, s, h, e)
    lse = lse.reshape(bn, dilation, h, L).transpose(0, 3, 1, 2).reshape(bn, s, h)
    return o, lse


def mixer_ab(h, positions, w_in, a_vnorm_g, a_spatial_w, a_spatial_b, q_norm_g, k_norm_g, w_out):
    bn, s, _ = h.shape
    z = h @ w_in
    ua, va, q, k, v = jnp.split(
        z, [A_WIDTH, 2 * A_WIDTH, 2 * A_WIDTH + B_WIDTH, 2 * A_WIDTH + 2 * B_WIDTH], axis=-1)
    nc = s // CHUNK
    ua = jax.nn.gelu(ua, approximate=False).reshape(bn, nc, CHUNK, A_GROUPS, A_GROUP_DIM)
    va = layer_norm(jax.nn.gelu(va, approximate=False).reshape(bn, s, A_GROUPS, A_GROUP_DIM), a_vnorm_g)
    va = va.reshape(bn, nc, CHUNK, A_GROUPS, A_GROUP_DIM)
    causal = jnp.tril(jnp.ones((CHUNK, CHUNK), dtype=bool))
    ws = jnp.where(causal[None], a_spatial_w, 0.0).astype(va.dtype)
    f = jnp.einsum("gts,bcsgd->bctgd", ws, va) + a_spatial_b.T[None, None, :, :, None]
    ya = (ua * f).reshape(bn, s, A_WIDTH)
    q = rotary(rms_norm(q.reshape(bn, s, B_HEADS, B_HEAD_DIM), q_norm_g), positions)
    k = rotary(rms_norm(k.reshape(bn, s, B_HEADS, B_HEAD_DIM), k_norm_g), positions)
    v = v.reshape(bn, s, B_HEADS, B_HEAD_DIM)
    outs, lses = [], []
    for window, dilation in DILATED_PATTERNS:
        o, lse = dilated_branch(q, k, v, window, dilation)
        outs.append(o)
        lses.append(lse)
    wts = jax.nn.softmax(jnp.stack(lses, 0), axis=0)
    yb = jnp.sum(wts[..., None] * jnp.stack(outs, 0), axis=0)
    yb = yb.astype(h.dtype).reshape(bn, s, B_WIDTH)
    return jnp.concatenate([ya, yb], axis=-1) @ w_out


def conformer_conv(h, pw1_w, pw1_b, dw_w, dw_b, ln_g, ln_b, pw2_w, pw2_b):
    a, g = jnp.split(h @ pw1_w + pw1_b, 2, axis=-1)
    y = a * jax.nn.sigmoid(g)
    y = causal_depthwise_conv(y, dw_w, dw_b)
    y = jax.nn.silu(layer_norm(y, ln_g) + ln_b)
    return y @ pw2_w + pw2_b


def conv_ffn(h, up_w, dw_w, dw_b, down_w):
    z = causal_depthwise_conv(h @ up_w, dw_w, dw_b)
    a, b = jnp.split(z, 2, axis=-1)
    return (jax.nn.silu(a) * b) @ down_w


def modulate(x, g, shift, scale):
    return rms_norm(x, g) * (1.0 + scale[:, None, :]) + shift[:, None, :]


def setup_inputs(seed: int = 0) -> dict:
    key = jax.random.key(seed)
    ks = jax.random.split(key, 32)
    D, F = D_MODEL, FFN_DIM

    def nrm(k, shape, scale):
        return jax.random.normal(k, shape, jnp.float32) * scale

    start = jax.random.randint(ks[2], (BATCH, 1), 0, 1024, dtype=jnp.int32)
    positions = (start + jnp.arange(SEQ, dtype=jnp.int32)[None, :]).astype(jnp.int32)
    return {
        "x": nrm(ks[0], (BATCH, SEQ, D), 1.0),
        "c": nrm(ks[1], (BATCH, D), 1.0),
        "positions": positions,
        "ada_w": nrm(ks[3], (DEPTH, D, 6 * D), 0.5 * D ** -0.5),
        "ada_b": nrm(ks[4], (DEPTH, 6 * D), 0.01),
        "norm_mix_g": 1.0 + nrm(ks[5], (DEPTH, D), 0.02),
        "norm_ffn_g": 1.0 + nrm(ks[6], (DEPTH, D), 0.02),
        "ab_w_in": nrm(ks[7], (N_EVEN, D, AB_IN), D ** -0.5),
        "a_vnorm_g": 1.0 + nrm(ks[8], (N_EVEN, A_GROUPS, A_GROUP_DIM), 0.02),
        "a_spatial_w": nrm(ks[9], (N_EVEN, A_GROUPS, CHUNK, CHUNK), CHUNK ** -0.5),
        "a_spatial_b": 1.0 + nrm(ks[10], (N_EVEN, A_GROUPS, CHUNK), 0.1),
        "b_q_norm_g": 1.0 + nrm(ks[11], (N_EVEN, B_HEAD_DIM), 0.02),
        "b_k_norm_g": 1.0 + nrm(ks[12], (N_EVEN, B_HEAD_DIM), 0.02),
        "ab_w_out": nrm(ks[13], (N_EVEN, A_WIDTH + B_WIDTH, D), (A_WIDTH + B_WIDTH) ** -0.5),
        "conv_pw1_w": nrm(ks[14], (N_ODD, D, 2 * D), D ** -0.5),
        "conv_pw1_b": nrm(ks[15], (N_ODD, 2 * D), 0.01),
        "conv_dw_w": nrm(ks[16], (N_ODD, CONV_WIDTH, D), CONV_WIDTH ** -0.5),
        "conv_dw_b": nrm(ks[17], (N_ODD, D), 0.01),
        "conv_ln_g": 1.0 + nrm(ks[18], (N_ODD, D), 0.02),
        "conv_ln_b": nrm(ks[19], (N_ODD, D), 0.01),
        "conv_pw2_w": nrm(ks[20], (N_ODD, D, D), D ** -0.5),
        "conv_pw2_b": nrm(ks[21], (N_ODD, D), 0.01),
        "ffn_up_w": nrm(ks[22], (DEPTH, D, 2 * F), D ** -0.5),
        "ffn_dw_w": nrm(ks[23], (DEPTH, FFN_CONV_WIDTH, 2 * F), FFN_CONV_WIDTH ** -0.5),
        "ffn_dw_b": nrm(ks[24], (DEPTH, 2 * F), 0.01),
        "ffn_down_w": nrm(ks[25], (DEPTH, F, D), F ** -0.5),
    }


def reference(x, c, positions, ada_w, ada_b, norm_mix_g, norm_ffn_g, ab_w_in, a_vnorm_g,
              a_spatial_w, a_spatial_b, b_q_norm_g, b_k_norm_g, ab_w_out, conv_pw1_w,
              conv_pw1_b, conv_dw_w, conv_dw_b, conv_ln_g, conv_ln_b, conv_pw2_w, conv_pw2_b,
              ffn_up_w, ffn_dw_w, ffn_dw_b, ffn_down_w):
    c_act = jax.nn.silu(c)
    for layer in range(DEPTH):
        mod = c_act @ ada_w[layer] + ada_b[layer]
        sh_m, sc_m, g_m, sh_f, sc_f, g_f = jnp.split(mod, 6, axis=-1)
        h = modulate(x, norm_mix_g[layer], sh_m, sc_m)
        li = layer // 2
        if layer % 2 == 0:
            y = mixer_ab(h, positions, ab_w_in[li], a_vnorm_g[li], a_spatial_w[li],
                         a_spatial_b[li], b_q_norm_g[li], b_k_norm_g[li], ab_w_out[li])
        else:
            y = conformer_conv(h, conv_pw1_w[li], conv_pw1_b[li], conv_dw_w[li], conv_dw_b[li],
                               conv_ln_g[li], conv_ln_b[li], conv_pw2_w[li], conv_pw2_b[li])
        x = x + g_m[:, None, :] * y
        h = modulate(x, norm_ffn_g[layer], sh_f, sc_f)
        x = x + g_f[:, None, :] * conv_ffn(h, ffn_up_w[layer], ffn_dw_w[layer],
                                           ffn_dw_b[layer], ffn_down_w[layer])
    return x
```

```python
import math
import numpy as np
import ml_dtypes
import concourse.bass as bass
import concourse.mybir as mybir
from concourse.bass_utils import run_bass_kernel_spmd
from contextlib import ExitStack

F32 = mybir.dt.float32
BF16 = mybir.dt.bfloat16
I32 = mybir.dt.int32
AF = mybir.ActivationFunctionType
ALU = mybir.AluOpType
AX = mybir.AxisListType

D = 1024
SEQ = 4096
T = 512
NT = SEQ // T
FF = 2816
NCH = 2 * FF // 128
EPS = 1e-6
NSLAB = 3
HV = 260


class Buf:
    __slots__ = ("name", "w", "r", "const")

    def __init__(self, name, const=False):
        self.name = name
        self.w = None
        self.r = []
        self.const = const


class Op:
    __slots__ = ("eng", "fn", "deps", "idx", "sig", "signo", "dma", "sem_key", "epoch")


class Sched:
    ENGS = ("pe", "act", "dve", "pool", "sp")
    NDMA = {"sp": 16, "pool": 6, "act": 4}

    def __init__(self):
        self.ops = []
        self.epoch = 0
        self.dma_count = {"sp": 0, "pool": 0, "act": 0}
        self.dma_last = {}

    def add(self, eng, fn, reads=(), writes=(), dma=False):
        i = len(self.ops)
        op = Op()
        op.eng, op.fn, op.idx, op.dma, op.sig, op.signo = eng, fn, i, dma, False, 0
        op.epoch = self.epoch
        deps = set()
        for b in reads:
            if b.w is not None:
                deps.add(b.w)
        for b in writes:
            if b.w is not None:
                deps.add(b.w)
            deps.update(b.r)
        for b in reads:
            if not b.const:
                b.r.append(i)
        for b in writes:
            b.w = i
            b.r = []
        deps.discard(i)
        if dma:
            k = self.dma_count[eng]
            self.dma_count[eng] = k + 1
            op.sem_key = ("dma", eng, k % self.NDMA[eng])
            prev = self.dma_last.get(op.sem_key)
            if prev is not None:
                deps.add(prev)
            self.dma_last[op.sem_key] = i
        else:
            op.sem_key = ("eng", eng, self.epoch)
        if eng == "pe" and not dma:
            deps = {d for d in deps if not (self.ops[d].eng == "pe" and not self.ops[d].dma)}
        best = {}
        out = set()
        for d in deps:
            o = self.ops[d]
            if o.dma:
                out.add(d)
            else:
                if o.sem_key not in best or best[o.sem_key] < d:
                    best[o.sem_key] = d
        out.update(best.values())
        op.deps = out
        self.ops.append(op)
        return i

    def emit(self, nc, final_deps=()):
        ops = self.ops
        for op in ops:
            for d in op.deps:
                ops[d].sig = True
        for d in final_deps:
            ops[d].sig = True
        counters = {}
        for op in ops:
            if op.sig:
                c = counters.get(op.sem_key, 0) + (16 if op.dma else 1)
                counters[op.sem_key] = c
                op.signo = c
        with ExitStack() as es:
            sems = {k: es.enter_context(nc.semaphore("s_" + "_".join(map(str, k)))) for k in counters}
            block = es.enter_context(nc.Block())
            per_eng = {e: [op for op in ops if op.eng == e] for e in self.ENGS}

            def body(eng_name, eng):
                seen = {}

                def wait_all(deps):
                    need = {}
                    for d in deps:
                        o = ops[d]
                        if need.get(o.sem_key, 0) < o.signo:
                            need[o.sem_key] = o.signo
                    for k, v in need.items():
                        if seen.get(k, 0) >= v:
                            continue
                        eng.wait_ge(sems[k], v)
                        seen[k] = v

                for op in per_eng[eng_name]:
                    wait_all(op.deps)
                    inst = op.fn(eng)
                    if op.sig:
                        inst.then_inc(sems[op.sem_key], 16 if op.dma else 1)
                if eng_name == "sp":
                    wait_all(final_deps)

            @block.tensor
            def _(e):
                body("pe", e)

            @block.scalar
            def _(e):
                body("act", e)

            @block.vector
            def _(e):
                body("dve", e)

            @block.gpsimd
            def _(e):
                body("pool", e)

            @block.sync
            def _(e):
                body("sp", e)


def _cols_layout():
    names = [("adab", 96), ("gmix", 16), ("gffn", 16), ("gq", 1), ("gk", 1), ("pw1b", 16), ("cdw", 248),
             ("cdb", 8), ("lng", 8), ("lnb", 8), ("pw2b", 8), ("fdw", 264), ("fdb", 88), ("invf", 1),
             ("eps", 1), ("halfpi", 1)]
    off = {}
    o = 0
    for n, w in names:
        off[n] = o
        o += w
    return off, o


COLS, NCOL = _cols_layout()
CB = {"bd64": 0, "rm": 128, "ones1024": 256, "m1a": 384, "m1b": 896, "m4p": 1408, "m4c": 1920,
      "m16_0": 2432, "m16_1": 2944, "m16_2": 3456, "m16_3": 3968, "m16o": 4480, "ident": 4992, "shift": 5120}
NCB = 5248


def _const_bf16():
    c = np.zeros((128, NCB), np.float32)
    p = np.arange(128)
    c[:, 0:128] = ((p[:, None] // 64) == (p[None, :] // 64)) * (1.0 / 64.0)
    rm = np.zeros((128, 128), np.float32)
    for m in range(128):
        if m % 64 < 32:
            rm[m + 32, m] = -1.0
        else:
            rm[m - 32, m] = 1.0
    c[:, 128:256] = rm
    c[:, 256:384] = 1.0 / 1024.0
    k = p[:, None]
    q = np.arange(256)[None, :]
    m1 = ((q - k >= 0) & (q - k <= 128)).astype(np.float32)
    c[:, 384:896] = np.concatenate([m1[:, 128:256], m1[:, 0:256], m1[:, 0:128]], 1)
    c[:, 896:1408] = np.concatenate([m1, m1], 1)
    c[:, 1408:1920] = np.tile(m1[:, 128:256], (1, 4))
    c[:, 1920:2432] = np.tile(m1[:, 0:128], (1, 4))
    j = np.arange(32)[None, :]
    for tt in range(4):
        ma = (k <= 32 * tt + j).astype(np.float32)
        c[:, 2432 + 512 * tt: 2432 + 512 * (tt + 1)] = np.tile(ma, (1, 16))
    mo = ((k >= j) & (k < 32)).astype(np.float32)
    c[:, 4480:4992] = np.tile(mo, (1, 16))
    c[:, 4992:5120] = np.eye(128, dtype=np.float32)
    sh = np.zeros((128, 128), np.float32)
    for kk in range(64):
        sh[kk, kk + 64] = 1.0
    c[:, 5120:5248] = sh
    return c.astype(ml_dtypes.bfloat16)


class KB:
    def __init__(self, nc):
        self.nc = nc
        self.S = Sched()
        self.es = ExitStack()
        self.rot = {}

    def sb(self, name, shape, dt):
        return self.es.enter_context(self.nc.sbuf_tensor("s_" + name, shape, dt))

    def ps(self, name, shape, dt):
        return self.es.enter_context(self.nc.psum_tensor("p_" + name, shape, dt))

    def add(self, eng, fn, r=(), w=(), dma=False):
        return self.S.add(eng, fn, r, w, dma)

    def mm(self, out, lhsT, rhs, start, stop, r, w, skip=False):
        if skip:
            return self.add("pe", lambda e: e.matmul(out, lhsT=lhsT, rhs=rhs, start=start, stop=stop,
                                                     skip_group_check=True), r, w)
        return self.add("pe", lambda e: e.matmul(out, lhsT=lhsT, rhs=rhs, start=start, stop=stop), r, w)

    def act(self, out, in_, func, r, w, scale=1.0, bias=None):
        if bias is None:
            return self.add("act", lambda e: e.activation(out=out, in_=in_, func=func, scale=scale), r, w)
        return self.add("act", lambda e: e.activation(out=out, in_=in_, func=func, scale=scale, bias=bias), r, w)

    def tt(self, out, in0, in1, op, r, w, eng="dve"):
        return self.add(eng, lambda e: e.tensor_tensor(out=out, in0=in0, in1=in1, op=op), r, w)

    def ts(self, out, in0, s1, s2, op0, op1, r, w, eng="dve"):
        if s2 is None:
            return self.add(eng, lambda e: e.tensor_scalar(out=out, in0=in0, scalar1=s1, scalar2=None, op0=op0), r, w)
        return self.add(eng, lambda e: e.tensor_scalar(out=out, in0=in0, scalar1=s1, scalar2=s2, op0=op0, op1=op1), r, w)

    def stt(self, out, in0, scalar, in1, op0, op1, r, w, eng="dve"):
        return self.add(eng, lambda e: e.scalar_tensor_tensor(out=out, in0=in0, scalar=scalar, in1=in1, op0=op0, op1=op1), r, w)

    def copy(self, out, in_, r, w, eng="dve"):
        return self.add(eng, lambda e: e.tensor_copy(out=out, in_=in_), r, w)

    def recip(self, out, in_, r, w):
        return self.add("dve", lambda e: e.reciprocal(out=out, in_=in_), r, w)

    def memset(self, ap, val, w, eng="dve"):
        return self.add(eng, lambda e: e.memset(ap, val), (), w)

    def dma(self, out, in_, r, w, eng="sp"):
        return self.add(eng, lambda e: e.dma_start(out=out, in_=in_), r, w, dma=True)

    def rr(self, key, n):
        i = self.rot.get(key, 0)
        self.rot[key] = i + 1
        return i % n


def build_program(nt=NT, dbg=False):
    nc = bass.Bass("TRN2", target_bir_lowering=False)
    kb = KB(nc)
    S = kb.S

    def din(name, shape, dt):
        return nc.dram_tensor(name, shape, dt, kind="ExternalInput").ap()

    def dscr(name, shape, dt):
        return nc.dram_tensor(name, shape, dt, kind="Internal").ap()

    xT = din("xT", [8, 128, SEQ], F32)
    cT = din("cT", [128, 8], F32)
    posr = din("posr", [128, SEQ], I32)
    ada_w = din("ada_w", [2, D, 6 * D], F32)
    w_in = din("w_in", [D, 2560], F32)
    w_out = din("w_out", [D, D], F32)
    up_w = din("up_w", [2, D, 2 * FF], F32)
    down_w = din("down_w", [2, FF, D], F32)
    pw1 = din("pw1", [D, 2 * D], F32)
    pw2 = din("pw2", [D, D], F32)
    wsT = din("wsT", [128, 4, 128], F32)
    cols_d = din("cols", [128, NCOL], F32)
    rows_d = din("rows", [128, 1024], F32)
    cbf_d = din("cbf", [128, NCB], BF16)
    wmask_d = din("wmask", [128, 128], F32)
    outT = nc.dram_tensor("outT", [8, 128, SEQ], F32, kind="ExternalOutput").ap()
    if dbg:
        dbgT = nc.dram_tensor("dbgT", [4, 8, 128, SEQ], F32, kind="ExternalOutput").ap()

    v_scr = dscr("v_scr", [SEQ, 2, HV], BF16)

    xt = kb.sb("xt", [128, 8, T], F32)
    b_xt = [Buf(f"xt{c}") for c in range(8)]
    hT = kb.sb("hT", [128, 8, T], BF16)
    b_hT = [Buf(f"hT{c}") for c in range(8)]
    slabs = [kb.sb(f"slab{i}", [128, 4096], BF16) for i in range(NSLAB)]
    b_slab = [[Buf(f"slab{i}a"), Buf(f"slab{i}b")] for i in range(NSLAB)]
    kT = kb.sb("kT", [128, 4, SEQ], BF16)
    b_kT = [[Buf(f"kT{c}_{t}") for t in range(NT)] for c in range(4)]
    qT = kb.sb("qT", [128, 4, T], BF16)
    b_qT = [Buf(f"qT{c}") for c in range(4)]
    yT = kb.sb("yT", [128, 8, T], BF16)
    b_yT = [Buf(f"yT{c}") for c in range(8)]
    ua = kb.sb("ua", [128, 4, T], F32)
    b_ua = [Buf(f"ua{c}") for c in range(4)]
    vn = kb.sb("vn", [128, 4, T], BF16)
    b_vn = [Buf(f"vn{c}") for c in range(4)]
    arena = kb.sb("arena", [128, 6656], F32)
    ar16 = arena[:, :].bitcast(BF16)
    V1 = ar16[:, 0:1300].rearrange("p (b c) -> p b c", b=5)
    V4 = ar16[:, 1300:3380].rearrange("p (b c) -> p b c", b=8)
    V16A = ar16[:, 3380:7540].rearrange("p (r c) -> p r c", r=16)
    V16O = ar16[:, 7540:11700].rearrange("p (r c) -> p r c", r=16)
    b_V1 = [Buf("V1")]
    b_V4 = [Buf("V4p"), Buf("V4c")]
    b_V16A = Buf("V16A")
    b_V16O = Buf("V16O")
    gT = ar16[:, 0:22 * T].rearrange("p (k t) -> p k t", k=22)
    b_gT = [Buf(f"gT{k}") for k in range(22)]
    yglu = ar16[:, 0:8 * 542].rearrange("p (c t) -> p c t", c=8)
    b_yglu = [Buf(f"yglu{c}") for c in range(8)]
    ycv = arena[:, 2560:2560 + 4096].rearrange("p (c t) -> p c t", c=8)
    b_ycv = [Buf(f"ycv{c}") for c in range(8)]
    arena_bufs = b_V1 + b_V4 + [b_V16A, b_V16O] + b_gT + b_yglu + b_ycv

    cbf = kb.sb("cbf", [128, NCB], BF16)
    b_cbf = Buf("cbf", const=True)
    cols = kb.sb("cols", [128, NCOL], F32)
    b_cols = Buf("cols", const=True)
    rows = kb.sb("rows", [128, 1024], F32)
    b_rows = Buf("rows", const=True)
    wsb = kb.sb("wsb", [128, 4, 128], BF16)
    b_wsb = Buf("wsb", const=True)
    onesf = kb.sb("onesf", [128, 128], F32)
    b_onesf = Buf("onesf", const=True)
    modc = kb.sb("modc", [128, 96], F32)
    b_modc = [Buf("modc0", const=True), Buf("modc1", const=True)]
    dcol = kb.sb("dcol", [128, 48], F32)
    b_dcol = [Buf("dcol0", const=True), Buf("dcol1", const=True)]
    cact = kb.sb("cact", [128, 8], BF16)
    b_cact = Buf("cact", const=True)
    NTF = 6
    tf = [kb.sb(f"tf{i}", [128, T], F32) for i in range(NTF)]
    b_tf = [Buf(f"tf{i}") for i in range(NTF)]
    NTB = 6
    tb = [kb.sb(f"tb{i}", [128, T], BF16) for i in range(NTB)]
    b_tb = [Buf(f"tb{i}") for i in range(NTB)]
    zs = [kb.sb(f"zs{i}", [128, T + 2], F32) for i in range(3)]
    b_zs = [Buf(f"zs{i}") for i in range(3)]
    b_zh = [Buf(f"zh{i}") for i in range(3)]
    acc = [kb.sb(f"acc{i}", [128, T], F32) for i in range(3)]
    b_acc = [Buf(f"acc{i}") for i in range(3)]
    sa = [kb.sb(f"sa{i}", [128, T], F32) for i in range(2)]
    b_sa = [Buf(f"sa{i}") for i in range(2)]
    halo = kb.sb("halo", [128, 2 * NCH * 2], F32)
    b_halo = [[Buf(f"halo{l}_{c}") for c in range(NCH)] for l in range(2)]
    vst = kb.sb("vst", [128, 4, 2 * HV], BF16)
    b_vst = [Buf(f"vst{i}") for i in range(4)]
    cst = kb.sb("cst", [128, 2, T], F32)
    b_cst = Buf("cst")
    posf = kb.sb("posf", [128, T], I32)
    b_posf = Buf("posf")
    yhalo = kb.sb("yhalo", [128, 8, 30], BF16)
    b_yhalo = [Buf(f"yhalo{c}") for c in range(8)]
    small = kb.sb("small", [128, 96], F32)
    b_small = Buf("small")
    b_s1 = [Buf(f"s1_{i}") for i in range(4)]
    b_s2 = [Buf(f"s2_{i}") for i in range(4)]
    b_sm, b_sv, b_snb = Buf("sm"), Buf("sv"), Buf("snb")
    rstd_t = kb.sb("rstd_t", [128, T], F32)
    b_rstd_t = Buf("rstd_t")
    ln_mu = kb.sb("ln_mu", [128, T], F32)
    b_ln_mu = Buf("ln_mu")
    rd, b_rd = ln_mu, b_ln_mu

    pbank = [kb.ps(f"pb{i}", [128, T], F32) for i in range(8)]
    b_pb = [Buf(f"pb{i}") for i in range(8)]
    PCL = {"A": [0, 1, 2], "B": [3, 4, 5], "C": [6], "D": [7], "O": [6, 7]}

    def psum(cl):
        lst = PCL[cl]
        i = lst[kb.rr("ps" + cl, len(lst))]
        return pbank[i], b_pb[i]

    def tmpf():
        i = kb.rr("tf", NTF)
        return tf[i], b_tf[i]

    def tmpb():
        i = kb.rr("tb", NTB)
        return tb[i], b_tb[i]

    def col(name, i=0, n=1):
        o = COLS[name] + i
        return cols[:, o:o + n]

    def cb(name, w=512):
        o = CB[name]
        return cbf[:, o:o + w]

    def load_x(t):
        for c in range(8):
            kb.dma(xt[:, c, :], xT[c, :, t * T:(t + 1) * T], [], [b_xt[c]])

    load_x(0)
    kb.dma(cbf[:, :], cbf_d, [], [b_cbf])
    kb.dma(cols[:, :], cols_d, [], [b_cols])
    kb.dma(rows[:, :], rows_d, [], [b_rows])
    ti, b_ti = tmpf()
    kb.dma(ti[:, 0:128], wmask_d, [], [b_ti])
    t2, b_t2 = tmpf()
    kb.dma(t2[:, :], wsT.rearrange("p g t -> p (g t)"), [], [b_t2])
    for g in range(4):
        kb.tt(wsb[:, g, :], t2[:, g * 128:(g + 1) * 128], ti[:, 0:128], ALU.mult, [b_t2, b_ti], [b_wsb])
    kb.memset(onesf[:, :], 1.0, [b_onesf])
    kb.memset(halo[:, :], 0.0, [b for l in b_halo for b in l])
    kb.memset(vst[:, :, :], 1.0, b_vst)
    b_w = {}

    t3, b_t3 = tmpf()
    kb.dma(t3[:, 0:8], cT, [], [b_t3])
    kb.act(cact[:, :], t3[:, 0:8], AF.Silu, [b_t3], [b_cact])
    slab_state = {"n": 0}

    def next_slab():
        i = slab_state["n"] % NSLAB
        slab_state["n"] += 1
        return slabs[i], b_slab[i]

    modp, b_modp = psum("C")

    def ada_slab(l, n):
        sl, b_sl = next_slab()
        kb.dma(sl[:, :].rearrange("p (k n) -> p k n", k=8),
               ada_w[l].rearrange("(k p) n -> p k n", p=128)[:, :, n * 512:(n + 1) * 512], [], b_sl, eng="pool")
        for jj in range(4):
            j = 4 * n + jj
            for kc in range(8):
                kb.mm(modp[:, l * 48 + j: l * 48 + j + 1],
                      sl[:, kc * 512 + jj * 128: kc * 512 + (jj + 1) * 128], cact[:, kc:kc + 1],
                      kc == 0, kc == 7, b_sl + [b_cact], [b_modp])

    def ada_finish(l):
        kb.tt(modc[:, l * 48:(l + 1) * 48], modp[:, l * 48:(l + 1) * 48], col("adab", l * 48, 48), ALU.add,
              [b_modp, b_cols], [b_modc[l]])
        for sub in range(2):
            o = (l * 2 + sub) * 8
            gname = "gmix" if sub == 0 else "gffn"
            kb.stt(dcol[:, o:o + 8], modc[:, l * 48 + (3 * sub + 1) * 8: l * 48 + (3 * sub + 1) * 8 + 8], 1.0,
                   col(gname, l * 8, 8), ALU.add, ALU.mult, [b_modc[l], b_cols], [b_dcol[l]])
        if l == 1:
            kb.tt(dcol[:, 32:40], col("pw2b", 0, 8), modc[:, 48 + 16: 48 + 24], ALU.mult, [b_modc[1], b_cols],
                  [b_dcol[1]])

    for l_ in range(2):
        for n in range(12):
            ada_slab(l_, n)
        ada_finish(l_)

    def gmcol(l, sub, c):
        o = (l * 2 + sub) * 8 + c
        return dcol[:, o:o + 1]

    def shcol(l, sub, c):
        o = l * 48 + (3 * sub) * 8 + c
        return modc[:, o:o + 1]

    def gatecol(l, sub, c):
        o = l * 48 + (3 * sub + 2) * 8 + c
        return modc[:, o:o + 1]

    wscr = dscr("wscr", [64, 128, 4096], BF16)
    slab_ids = {}
    b_wscr = {}

    def get_w(key, parts, build=None):
        sl, b_sl = next_slab()
        used = max([k * ncols for (k, ncols, c0, w, src, bi) in parts] + ([31 * 128] if build is not None else []))
        if key not in slab_ids:
            sid = len(slab_ids)
            slab_ids[key] = sid
            if build is not None:
                build(sl, b_sl)
            for (k, ncols, c0, w, src, bi) in parts:
                view = sl[:, 0:k * ncols].rearrange("p (k n) -> p k n", k=k)[:, :, c0:c0 + w]
                kb.dma(view, src, [], b_sl if bi is None else [b_sl[bi]], eng="pool")
            b = Buf(f"wscr{sid}")
            b_wscr[sid] = b
            kb.dma(wscr[sid][:, 0:used], sl[:, 0:used], b_sl, [b], eng="sp")
        else:
            sid = slab_ids[key]
            kb.dma(sl[:, 0:used], wscr[sid][:, 0:used], [b_wscr[sid]], b_sl, eng="sp")
        return sl, b_sl

    def w_kpn(w2d):
        return w2d.rearrange("(k p) n -> p k n", p=128)

    def norm_mod(l, sub):
        ssb, b_ssb = psum("C")
        for c in range(8):
            sq, b_sq = tmpb()
            kb.act(sq[:, :], xt[:, c, :], AF.Square, [b_xt[c]], [b_sq])
            kb.mm(ssb[:, :], cb("ones1024", 128), sq[:, :], c == 0, c == 7, [b_sq, b_cbf], [b_ssb])
        sd, b_sd = tmpf()
        kb.act(sd[:, :], ssb[:, :], AF.Ln, [b_ssb, b_cols], [b_sd], bias=col("eps"))
        rs, b_rs = rstd_t, b_rstd_t
        kb.act(rs[:, :], sd[:, :], AF.Exp, [b_sd], [b_rs], scale=-0.5)
        for c in range(8):
            t, b_t = tmpf()
            kb.stt(t[:, :], xt[:, c, :], gmcol(l, sub, c), rs[:, :], ALU.mult, ALU.mult,
                   [b_xt[c], b_rs, b_dcol[l]], [b_t])
            kb.act(hT[:, c, :], t[:, :], AF.Identity, [b_t, b_modc[l]], [b_hT[c]], bias=shcol(l, sub, c))

    def proj_fm(sl, b_sl, ncols, col0, src, b_src, nk=8):
        bank, b_bank = psum("A")
        for kc in range(nk):
            kb.mm(bank[:, :], sl[:, kc * ncols + col0: kc * ncols + col0 + 128], src[:, kc, :],
                  kc == 0, kc == nk - 1, b_sl + [b_src[kc]], [b_bank])
        return bank, b_bank

    def proj_tm(sl, b_sl, tc, src, b_src):
        bank, b_bank = psum("A")
        for kc in range(8):
            kb.mm(bank[:, :], src[:, kc, tc * 128:(tc + 1) * 128], sl[:, kc * 512:(kc + 1) * 512],
                  kc == 0, kc == 7, b_sl + [b_src[kc]], [b_bank])
        return bank, b_bank

    def arena_barrier():
        kb.memset(small[:, 90:91], 0.0, [b_small] + arena_bufs)

    def dbg_dump(stage, t):
        if dbg:
            for c in range(8):
                kb.dma(dbgT[stage, c, :, t * T:(t + 1) * T], xt[:, c, :], [b_xt[c]], [])

    def qk_stages(sl, b_sl, fc, gname, dst, b_dst):
        st = {}

        def sA():
            st["qp"] = proj_fm(sl, b_sl, 512, fc * 128, hT, b_hT)
            qp, b_qp = st["qp"]
            st["sq"] = tmpb()
            sq, b_sq = st["sq"]
            kb.act(sq[:, :], qp[:, :], AF.Square, [b_qp], [b_sq])

        def sB():
            qp, b_qp = st["qp"]
            sq, b_sq = st["sq"]
            ssb, b_ssb = psum("C")
            kb.mm(ssb[:, :], cb("bd64", 128), sq[:, :], True, True, [b_sq, b_cbf], [b_ssb])
            sd, b_sd = tmpf()
            kb.act(sd[:, :], ssb[:, :], AF.Ln, [b_ssb, b_cols], [b_sd], bias=col("eps"))
            rs, b_rs = tmpf()
            kb.act(rs[:, :], sd[:, :], AF.Exp, [b_sd], [b_rs], scale=-0.5)
            st["q1"] = tmpb()
            q1, b_q1 = st["q1"]
            kb.stt(q1[:, :], qp[:, :], col(gname), rs[:, :], ALU.mult, ALU.mult, [b_qp, b_rs, b_cols], [b_q1])

        def sC():
            q1, b_q1 = st["q1"]
            rot, b_rot = psum("D")
            kb.mm(rot[:, :], cb("rm", 128), q1[:, :], True, True, [b_q1, b_cbf], [b_rot])
            t1, b_t1 = tmpf()
            kb.tt(t1[:, :], q1[:, :], cst[:, 0, :], ALU.mult, [b_q1, b_cst], [b_t1])
            t2_, b_t2_ = tmpf()
            kb.tt(t2_[:, :], rot[:, :], cst[:, 1, :], ALU.mult, [b_rot, b_cst], [b_t2_])
            kb.tt(dst, t1[:, :], t2_[:, :], ALU.add, [b_t1, b_t2_], [b_dst])

        return [sA, sB, sC]

    def l0_mixer(t):
        arena_barrier()
        norm_mod(0, 0)
        kb.dma(posf[:, :], posr[:, t * T:(t + 1) * T], [], [b_posf])
        ang, b_ang = tmpf()
        kb.copy(ang[:, :], posf[:, :], [b_posf], [b_ang])
        kb.ts(ang[:, :], ang[:, :], col("invf"), None, ALU.mult, None, [b_ang, b_cols], [b_ang])
        u_, b_u = tmpf()
        kb.ts(u_[:, :], ang[:, :], 1.0 / (2.0 * math.pi), None, ALU.mult, None, [b_ang], [b_u])
        kb.copy(posf[:, :], u_[:, :], [b_u], [b_posf])
        kb.copy(u_[:, :], posf[:, :], [b_posf], [b_u])
        r_, b_r = tmpf()
        kb.stt(r_[:, :], u_[:, :], -2.0 * math.pi, ang[:, :], ALU.mult, ALU.add, [b_u, b_ang], [b_r])
        sh_, b_sh = tmpf()
        kb.act(sh_[:, :], r_[:, :], AF.Sin, [b_r], [b_sh], scale=0.5)
        ch_, b_ch = tmpf()
        kb.act(ch_[:, :], r_[:, :], AF.Sin, [b_r, b_cols], [b_ch], scale=-0.5, bias=col("halfpi"))
        kb.stt(cst[:, 1, :], sh_[:, :], 2.0, ch_[:, :], ALU.mult, ALU.mult, [b_sh, b_ch], [b_cst])
        kb.tt(sh_[:, :], sh_[:, :], sh_[:, :], ALU.mult, [b_sh], [b_sh])
        kb.ts(cst[:, 0, :], sh_[:, :], -2.0, 1.0, ALU.mult, ALU.add, [b_sh], [b_cst])

        wk = w_kpn(w_in)
        sl, b_sl = get_w("win0", [(8, 512, 0, 512, wk[:, :, 0:512], None)])
        for fc in range(4):
            bank, b_bank = proj_fm(sl, b_sl, 512, fc * 128, hT, b_hT)
            kb.act(ua[:, fc, :], bank[:, :], AF.Gelu, [b_bank], [b_ua[fc]])
        sl, b_sl = get_w("win512", [(8, 512, 0, 512, wk[:, :, 512:1024], None)])
        S1, S2, M_, V_ = small[:, 0:16], small[:, 16:32], small[:, 32:48], small[:, 48:64]
        kb.memset(S2, 0.0, b_s2)
        vgs = []
        for tc in range(4):
            bank, b_bank = proj_tm(sl, b_sl, tc, hT, b_hT)
            vg, b_vg = tmpf()
            kb.act(vg[:, :], bank[:, :], AF.Gelu, [b_bank], [b_vg])
            kb.add("dve", lambda e, tc=tc, vg=vg: e.tensor_reduce(
                out=small[:, tc * 4:(tc + 1) * 4], in_=vg[:, :].rearrange("p (g d) -> p g d", g=4), axis=AX.X,
                op=ALU.add), [b_vg], [b_s1[tc]])
            for g in range(4):
                junk, b_junk = tmpb()
                kb.add("act", lambda e, tc=tc, g=g, vg=vg, junk=junk: e.activation(
                    out=junk[:, 0:128], in_=vg[:, g * 128:(g + 1) * 128], func=AF.Square,
                    accum_out=small[:, 16 + tc * 4 + g: 17 + tc * 4 + g]), [b_vg], [b_junk, b_s2[tc]])
            vgs.append((vg, b_vg))
        kb.ts(M_, S1, 1.0 / 128.0, None, ALU.mult, None, b_s1, [b_sm])
        kb.tt(V_, M_, M_, ALU.mult, [b_sm], [b_sv])
        kb.stt(V_, S2, 1.0 / 128.0, V_, ALU.mult, ALU.subtract, b_s2 + [b_sv], [b_sv])
        kb.act(V_, V_, AF.Ln, [b_sv, b_cols], [b_sv], bias=col("eps"))
        kb.act(V_, V_, AF.Exp, [b_sv], [b_sv], scale=-0.5)
        kb.stt(S1, M_, -1.0, V_, ALU.mult, ALU.mult, [b_sm, b_sv] + b_s1, [b_snb])
        for tc in range(4):
            vg, b_vg = vgs[tc]
            for g in range(4):
                o = tc * 4 + g
                kb.act(vg[:, g * 128:(g + 1) * 128], vg[:, g * 128:(g + 1) * 128], AF.Identity,
                       [b_vg, b_sv, b_snb], [b_vg], scale=small[:, 48 + o:49 + o], bias=small[:, o:o + 1])
            kb.tt(vn[:, tc, :], vg[:, :], rows[:, 0:512], ALU.mult, [b_vg, b_rows], [b_vn[tc]])
        slq, b_slq = get_w("win1024", [(8, 512, 0, 512, wk[:, :, 1024:1536], None)])
        slk, b_slk = get_w("win1536", [(8, 512, 0, 512, wk[:, :, 1536:2048], None)])
        stg = [qk_stages(slq, b_slq, fc, "gq", qT[:, fc, :], b_qT[fc]) for fc in range(4)]
        stg += [qk_stages(slk, b_slk, fc, "gk", kT[:, fc, t * T:(t + 1) * T], b_kT[fc][t]) for fc in range(4)]
        n_ = len(stg)
        for i in range(n_ + 2):
            if i < n_:
                stg[i][0]()
            if 0 <= i - 1 < n_:
                stg[i - 1][1]()
            if 0 <= i - 2 < n_:
                stg[i - 2][2]()
        sl, b_sl = get_w("win2048", [(8, 512, 0, 512, wk[:, :, 2048:2560], None)])
        b_vrow = []
        for tc in range(4):
            bank, b_bank = proj_tm(sl, b_sl, tc, hT, b_hT)
            kb.act(vst[:, tc, :].rearrange("p (h e) -> p h e", e=65)[:, :, 0:64],
                   bank[:, :].rearrange("p (h e) -> p h e", e=64), AF.Identity, [b_bank], [b_vst[tc]])
            b = Buf(f"vrow{t}_{tc}")
            kb.dma(v_scr[t * T + tc * 128: t * T + (tc + 1) * 128].rearrange("p a c -> p (a c)"), vst[:, tc, :],
                   [b_vst[tc]], [b])
            b_vrow.append(b)
        b_vtile[t] = b_vrow

        ks = max(0, 32 * (t - 3))
        nk = 32 * (t + 1) - ks

        def head_tasks(hl, h):
            c = h // 2
            pb = 64 * (h % 2)
            vc = slice(hl * 65, hl * 65 + 65)
            O, b_O = psum("O")
            first = [True]
            kTc = kT[pb:pb + 64, c, :]
            qTc = qT[pb:pb + 64, c, :]
            rdq = [b_qT[c]]
            tasks = []

            def pv(ocols, lhsT, rhs, rds):
                if isinstance(ocols, tuple):
                    oap = O[0:65, :].rearrange("p (i s) -> p i s", s=ocols[1])[:, :, ocols[0]]
                else:
                    oap = O[0:65, ocols]
                kb.mm(oap, lhsT, rhs, first[0], False, rds, [b_O], skip=True)
                first[0] = False

            def add_task(qk_list, np_, c0, mask, pv_list):
                st = {}

                def qk():
                    Sb, b_Sb = psum("B")
                    st["S"] = (Sb, b_Sb)
                    for (oc, lh, rh, rds) in qk_list:
                        kb.mm(Sb[0:np_, oc], lh, rh, True, True, rds, [b_Sb])

                def post():
                    Sb, b_Sb = st["S"]
                    P, b_P = tmpb()
                    kb.act(P[0:np_, c0:512], Sb[0:np_, c0:512], AF.Exp, [b_Sb], [b_P], scale=0.125)
                    kb.tt(P[0:np_, c0:512], P[0:np_, c0:512], mask[0:np_, c0:512], ALU.mult, [b_P, b_cbf], [b_P])
                    for (ocols, lh, pc, rds) in pv_list:
                        pv(ocols, lh, P[0:np_, pc], [b_P] + rds)

                tasks.append([qk, post])

            units1 = [(4 * t - 1, 0, 0, 128), (4 * t, 128, 0, 256), (4 * t + 3, 384, 384, 128)]
            units2 = [(4 * t + 1, 0, 128, 256), (4 * t + 2, 256, 256, 256)]
            for ui, units in enumerate((units1, units2)):
                units = [u for u in units if u[0] >= 0]
                lo = min(u[1] for u in units)
                qk_list = [(slice(bc, bc + w), kTc[:, kbk * 128:(kbk + 1) * 128], qTc[:, q0:q0 + w],
                            [b_kT[c][kbk // 4]] + rdq) for (kbk, bc, q0, w) in units]
                pv_list = [(slice(q0, q0 + w), V1[:, kbk - (4 * t - 1), vc], slice(bc, bc + w), [b_V1[0]])
                           for (kbk, bc, q0, w) in units]
                add_task(qk_list, 128, lo, cb("m1a" if ui == 0 else "m1b"), pv_list)
            for which in (0, 1):
                tk = t - 1 + which
                if tk < 0:
                    continue
                qk_list = [(slice(r * 128, (r + 1) * 128),
                            kTc[:, tk * T:(tk + 1) * T].rearrange("p (i s) -> p i s", s=4)[:, :, r],
                            qTc.rearrange("p (i s) -> p i s", s=4)[:, :, r], [b_kT[c][tk]] + rdq) for r in range(4)]
                pv_list = [((r, 4), V4[:, 4 * which + r, vc], slice(r * 128, (r + 1) * 128), [b_V4[which]])
                           for r in range(4)]
                add_task(qk_list, 128, 0, cb("m4p" if which == 0 else "m4c"), pv_list)
            rdk = [b_kT[c][tt_] for tt_ in range(max(0, t - 3), t + 1)]
            qk_list = [(slice(r * 32, (r + 1) * 32),
                        kTc[:, 16 * ks:16 * (ks + nk)].rearrange("p (i s) -> p i s", s=16)[:, :, r],
                        qTc.rearrange("p (i s) -> p i s", s=16)[:, :, r], rdk + rdq) for r in range(16)]
            pv_list = [((r, 16), V16A[0:nk, r, vc], slice(r * 32, (r + 1) * 32), [b_V16A]) for r in range(16)]
            add_task(qk_list, nk, 0, cb(f"m16_{min(t, 3)}"), pv_list)
            if t >= 4:
                qk_list = [(slice(r * 32, (r + 1) * 32),
                            kTc[:, T * (t - 4):T * (t - 3)].rearrange("p (i s) -> p i s", s=16)[:, :, r],
                            qTc.rearrange("p (i s) -> p i s", s=16)[:, :, r], [b_kT[c][t - 4]] + rdq)
                           for r in range(16)]
                pv_list = [((r, 16), V16O[0:32, r, vc], slice(r * 32, (r + 1) * 32), [b_V16O]) for r in range(16)]
                add_task(qk_list, 32, 0, cb("m16o"), pv_list)

            def finish():
                kb.act(rd[64:65, :], O[64:65, :], AF.Ln, [b_O], [b_rd])
                kb.act(rd[64:65, :], rd[64:65, :], AF.Exp, [b_rd], [b_rd], scale=-1.0)
                BC, b_BC = psum("A")
                kb.mm(BC[0:64, :], onesf[64:65, 0:64], rd[64:65, :], True, True, [b_rd, b_onesf], [b_BC])
                on, b_on = tmpf()
                kb.act(on[0:64, :], O[0:64, :], AF.Identity, [b_O], [b_on])
                if pb == 0:
                    kb.tt(yT[0:64, 4 + c, :], on[0:64, :], BC[0:64, :], ALU.mult, [b_on, b_BC], [b_yT[4 + c]])
                else:
                    yb, b_yb = tmpb()
                    kb.tt(yb[0:64, :], on[0:64, :], BC[0:64, :], ALU.mult, [b_on, b_BC], [b_yb])
                    SH, b_SH = psum("A")
                    kb.mm(SH[:, :], cbf[0:64, CB["shift"]:CB["shift"] + 128], yb[0:64, :], True, True,
                          [b_yb, b_cbf], [b_SH])
                    kb.copy(yT[64:128, 4 + c, :], SH[64:128, :], [b_SH], [b_yT[4 + c]])

            tasks[-1].append(finish)
            return tasks

        for hh in range(2):
            rd2 = b_vtile[t] + (b_vtile[t - 1] if t > 0 else [])
            b0 = 1 if t == 0 else 0
            kb.dma(V1[:, b0:5, :], v_scr[(4 * t - 1 + b0) * 128:(t + 1) * T, hh, :].rearrange("(b p) c -> p b c", p=128),
                   rd2, b_V1)
            if t > 0:
                kb.dma(V4[:, 0:4, :], v_scr[(t - 1) * T:t * T, hh, :].rearrange("(p r) c -> p r c", r=4),
                       b_vtile[t - 1], [b_V4[0]])
            kb.dma(V4[:, 4:8, :], v_scr[t * T:(t + 1) * T, hh, :].rearrange("(p r) c -> p r c", r=4),
                   b_vtile[t], [b_V4[1]])
            rd_v = [b for tt_ in range(max(0, t - 3), t + 1) for b in b_vtile[tt_]]
            kb.dma(V16A[0:nk, :, :], v_scr[16 * ks:16 * (ks + nk), hh, :].rearrange("(p r) c -> p r c", r=16),
                   rd_v, [b_V16A])
            if t >= 4:
                kb.dma(V16O[0:32, :, :], v_scr[T * (t - 4):T * (t - 3), hh, :].rearrange("(p r) c -> p r c", r=16),
                       b_vtile[t - 4], [b_V16O])
            tasks = []
            for hl in range(4):
                tasks += head_tasks(hl, hh * 4 + hl)
            LOOK = 2
            for i in range(min(LOOK, len(tasks))):
                tasks[i][0]()
            DEFER = 2
            for i in range(len(tasks) + DEFER):
                if i + LOOK < len(tasks):
                    tasks[i + LOOK][0]()
                if i < len(tasks):
                    tasks[i][1]()
                if 0 <= i - DEFER < len(tasks) and len(tasks[i - DEFER]) > 2:
                    tasks[i - DEFER][2]()
        for g in range(4):
            bank, b_bank = psum("A")
            for tc in range(4):
                kb.mm(bank[:, tc * 128:(tc + 1) * 128], vn[:, tc, g * 128:(g + 1) * 128], wsb[:, g, :], True, True,
                      [b_vn[tc], b_wsb], [b_bank])
            f, b_f = tmpf()
            for tc in range(4):
                kb.tt(f[:, tc * 128:(tc + 1) * 128], bank[:, tc * 128:(tc + 1) * 128],
                      rows[:, 512 + g * 128: 512 + (g + 1) * 128], ALU.add, [b_bank, b_rows], [b_f])
            kb.tt(yT[:, g, :], f[:, :], ua[:, g, :], ALU.mult, [b_f, b_ua[g]], [b_yT[g]])
        wo = w_kpn(w_out)
        for half in range(2):
            sl, b_sl = get_w(f"wout{half}", [(8, 512, 0, 512, wo[:, :, half * 512:(half + 1) * 512], None)])
            for fcl in range(4):
                fc = half * 4 + fcl
                bank, b_bank = proj_fm(sl, b_sl, 512, fcl * 128, yT, b_yT)
                kb.stt(xt[:, fc, :], bank[:, :], gatecol(0, 0, fc), xt[:, fc, :], ALU.mult, ALU.add,
                       [b_bank, b_modc[0], b_xt[fc]], [b_xt[fc]])

    b_vtile = {}

    def ffn(l, t):
        arena_barrier()
        norm_mod(l, 1)
        wk = w_kpn(up_w[l])
        wname = f"up{l}"
        pend = []
        sas = {}
        for pr in range(11):
            sl, b_sl = get_w(f"{wname}_{pr}", [(8, 512, 0, 256, wk[:, :, pr * 256:(pr + 1) * 256], 0),
                                               (8, 512, 256, 256, wk[:, :, FF + pr * 256: FF + (pr + 1) * 256], 1)])
            chans = [2 * pr, 2 * pr + 1, 22 + 2 * pr, 22 + 2 * pr + 1]
            for ci, ch in enumerate(chans):
                bank, b_bank = proj_fm(sl, [b_sl[ci // 2]], 512, ci * 128, hT, b_hT)
                zi = kb.rr("zs", 3)
                z, b_z = zs[zi], b_zs[zi]
                ai = kb.rr("acc", 3)
                a, b_a = acc[ai], b_acc[ai]
                hcol = halo[:, (l * NCH + ch) * 2:(l * NCH + ch) * 2 + 2]
                kb.act(z[:, 2:T + 2], bank[:, :], AF.Identity, [b_bank], [b_z])
                kb.act(z[:, 0:2], hcol, AF.Identity, [b_halo[l][ch]], [b_zh[zi]])
                wo_ = COLS["fdw"] + (l * NCH + ch) * 3
                kb.act(a[:, :], bank[:, :], AF.Identity, [b_bank, b_cols], [b_a], scale=cols[:, wo_ + 2:wo_ + 3],
                       bias=col("fdb", l * NCH + ch))
                kb.act(hcol, bank[:, T - 2:T], AF.Identity, [b_bank], [b_halo[l][ch]])
                kb.stt(a[:, :], z[:, 1:T + 1], cols[:, wo_ + 1:wo_ + 2], a[:, :], ALU.mult, ALU.add,
                       [b_z, b_zh[zi], b_a, b_cols], [b_a])
                kb.stt(a[:, :], z[:, 0:T], cols[:, wo_:wo_ + 1], a[:, :], ALU.mult, ALU.add,
                       [b_z, b_zh[zi], b_a, b_cols], [b_a])
                if pend:
                    pend.pop(0)()

                def tail(ci=ci, a=a, b_a=b_a, pr=pr):
                    if ci < 2:
                        si = kb.rr("sa", 2)
                        kb.act(sa[si][:, :], a[:, :], AF.Silu, [b_a], [b_sa[si]])
                        sas[ci] = si
                    else:
                        si = sas[ci - 2]
                        k_ = 2 * pr + (ci - 2)
                        kb.tt(gT[:, k_, :], sa[si][:, :], a[:, :], ALU.mult, [b_sa[si], b_a], [b_gT[k_]])

                pend.append(tail)
        while pend:
            pend.pop(0)()
        wd = down_w[l].rearrange("(k p) n -> p k n", p=128)
        dname = f"down{l}"
        for cp in range(4):
            banks = [psum("A") for _ in range(2)]
            for kh in range(2):
                sl, b_sl = get_w(f"{dname}_{cp}_{kh}", [(11, 256, 0, 256, wd[:, kh * 11:(kh + 1) * 11, cp * 256:(cp + 1) * 256], None)])
                for fl in range(2):
                    bank, b_bank = banks[fl]
                    for kk in range(11):
                        k_ = kh * 11 + kk
                        kb.mm(bank[:, :], sl[:, kk * 256 + fl * 128: kk * 256 + (fl + 1) * 128], gT[:, k_, :],
                              k_ == 0, k_ == 21, b_sl + [b_gT[k_]], [b_bank])
            for fl in range(2):
                fc = cp * 2 + fl
                bank, b_bank = banks[fl]
                kb.stt(xt[:, fc, :], bank[:, :], gatecol(l, 1, fc), xt[:, fc, :], ALU.mult, ALU.add,
                       [b_bank, b_modc[l], b_xt[fc]], [b_xt[fc]])

    def l1_mixer(t):
        arena_barrier()
        norm_mod(1, 0)
        wk = w_kpn(pw1)
        for c in range(8):
            if t == 0:
                kb.memset(yglu[:, c, 0:30], 0.0, [b_yglu[c]])
            else:
                kb.copy(yglu[:, c, 0:30], yhalo[:, c, :], [b_yhalo[c]], [b_yglu[c]])
        for pr in range(4):
            sl, b_sl = get_w(f"pw1_{pr}", [(8, 512, 0, 256, wk[:, :, pr * 256:(pr + 1) * 256], 0),
                                           (8, 512, 256, 256, wk[:, :, D + pr * 256: D + (pr + 1) * 256], 1)])
            sgs = []
            for ci in (2, 3, 0, 1):
                bank, b_bank = proj_fm(sl, [b_sl[ci // 2]], 512, ci * 128, hT, b_hT)
                if ci >= 2:
                    ch = 8 + 2 * pr + (ci - 2)
                    sg, b_sg = tmpf()
                    kb.act(sg[:, :], bank[:, :], AF.Sigmoid, [b_bank, b_cols], [b_sg], bias=col("pw1b", ch))
                    sgs.append((sg, b_sg))
                else:
                    ch = 2 * pr + ci
                    sg, b_sg = sgs[ci]
                    kb.stt(yglu[:, ch, 30:30 + T], bank[:, :], col("pw1b", ch), sg[:, :], ALU.add, ALU.mult,
                           [b_bank, b_sg, b_cols], [b_yglu[ch]])
        for fc in range(8):
            kb.ts(xt[:, fc, :], xt[:, fc, :], dcol[:, 32 + fc:33 + fc], None, ALU.add, None,
                  [b_xt[fc], b_dcol[1]], [b_xt[fc]])
        mean, b_mean = psum("C")
        ex2, b_ex2 = psum("D")
        for c in range(8):
            def build_dg(sl_, b_sl_, c=c):
                kb.tt(sl_[:, 0:31 * 128].rearrange("p (j m) -> p j m", j=31),
                      cb("ident", 128).unsqueeze(1).to_broadcast([128, 31, 128]),
                      col("cdw", c * 31, 31).unsqueeze(2).to_broadcast([128, 31, 128]), ALU.mult,
                      [b_cbf, b_cols], b_sl_)

            sl, b_sl = get_w(f"dg{c}", [], build=build_dg)
            bank, b_bank = psum("A")
            for j in range(31):
                kb.mm(bank[:, :], sl[:, j * 128:(j + 1) * 128], yglu[:, c, j:j + T], j == 0, j == 30,
                      b_sl + [b_yglu[c]], [b_bank])
            kb.copy(yhalo[:, c, :], yglu[:, c, T:T + 30], [b_yglu[c]], [b_yhalo[c]])
            kb.act(ycv[:, c, :], bank[:, :], AF.Identity, [b_bank, b_cols], [b_ycv[c]], bias=col("cdb", c))
            ycb, b_ycb = tmpb()
            kb.act(ycb[:, :], bank[:, :], AF.Identity, [b_bank, b_cols], [b_ycb], bias=col("cdb", c))
            sq, b_sq = tmpb()
            kb.act(sq[:, :], bank[:, :], AF.Square, [b_bank, b_cols], [b_sq], bias=col("cdb", c))
            kb.mm(mean[:, :], cb("ones1024", 128), ycb[:, :], c == 0, c == 7, [b_ycb, b_cbf], [b_mean])
            kb.mm(ex2[:, :], cb("ones1024", 128), sq[:, :], c == 0, c == 7, [b_sq, b_cbf], [b_ex2])
        mu, b_mu = ln_mu, b_ln_mu
        kb.act(mu[:, :], mean[:, :], AF.Identity, [b_mean], [b_mu])
        msq, b_msq = tmpf()
        kb.tt(msq[:, :], mu[:, :], mu[:, :], ALU.mult, [b_mu], [b_msq])
        var, b_var = rstd_t, b_rstd_t
        kb.tt(var[:, :], ex2[:, :], msq[:, :], ALU.subtract, [b_ex2, b_msq], [b_var])
        kb.act(var[:, :], var[:, :], AF.Ln, [b_var, b_cols], [b_var], bias=col("eps"))
        kb.act(var[:, :], var[:, :], AF.Exp, [b_var], [b_var], scale=-0.5)
        for c in range(8):
            d_, b_d = tmpf()
            kb.tt(d_[:, :], ycv[:, c, :], mu[:, :], ALU.subtract, [b_ycv[c], b_mu], [b_d])
            kb.tt(d_[:, :], d_[:, :], var[:, :], ALU.mult, [b_d, b_var], [b_d])
            kb.act(hT[:, c, :], d_[:, :], AF.Silu, [b_d, b_cols], [b_hT[c]], scale=col("lng", c), bias=col("lnb", c))
        wp = w_kpn(pw2)
        for half in range(2):
            sl, b_sl = get_w(f"pw2_{half}", [(8, 512, 0, 512, wp[:, :, half * 512:(half + 1) * 512], None)])
            for fcl in range(4):
                fc = half * 4 + fcl
                bank, b_bank = proj_fm(sl, b_sl, 512, fcl * 128, hT, b_hT)
                kb.stt(xt[:, fc, :], bank[:, :], gatecol(1, 0, fc), xt[:, fc, :], ALU.mult, ALU.add,
                       [b_bank, b_modc[1], b_xt[fc]], [b_xt[fc]])

    outs = []
    for t in range(nt):
        S.epoch = t + 1
        l0_mixer(t)
        dbg_dump(0, t)
        if dbg and t == 0:
            for c in range(8):
                kb.dma(dbgT[3, c, :, 0:T], hT[:, c, :], [b_hT[c]], [])
                kb.dma(dbgT[3, c, :, T:2 * T], yT[:, c, :], [b_yT[c]], [])
            for c in range(4):
                kb.dma(dbgT[3, c, :, 2 * T:3 * T], qT[:, c, :], [b_qT[c]], [])
                kb.dma(dbgT[3, c, :, 3 * T:4 * T], kT[:, c, 0:T], [b_kT[c][0]], [])
            kb.dma(dbgT[3, 4, :, 2 * T:3 * T], cst[:, 0, :], [b_cst], [])
            kb.dma(dbgT[3, 5, :, 2 * T:3 * T], cst[:, 1, :], [b_cst], [])
        ffn(0, t)
        dbg_dump(1, t)
        l1_mixer(t)
        dbg_dump(2, t)
        ffn(1, t)
        for c in range(8):
            outs.append(kb.dma(outT[c, :, t * T:(t + 1) * T], xt[:, c, :], [b_xt[c]], []))
        if t + 1 < nt:
            load_x(t + 1)
    S.emit(nc, final_deps=outs)
    _CACHE["sbuf_left"] = nc.sbuf_bytes_remaining
    kb.es.close()
    return nc


_CACHE = {}


def _prep_shared(inp):
    f = np.float32
    cols = np.zeros((128, NCOL), f)

    def put(name, arr):
        cols[:, COLS[name]:COLS[name] + arr.shape[1]] = arr

    def pc(v):
        return np.ascontiguousarray(v.reshape(-1, 128).T)

    put("adab", np.concatenate([pc(inp["ada_b"][l]) for l in range(2)], 1))
    put("gmix", np.concatenate([pc(inp["norm_mix_g"][l]) for l in range(2)], 1))
    put("gffn", np.concatenate([pc(inp["norm_ffn_g"][l]) for l in range(2)], 1))
    put("gq", np.tile(inp["b_q_norm_g"][0], 2)[:, None])
    put("gk", np.tile(inp["b_k_norm_g"][0], 2)[:, None])
    put("pw1b", pc(inp["conv_pw1_b"][0]))
    cdw = inp["conv_dw_w"][0]
    put("cdw", np.ascontiguousarray(cdw.reshape(31, 8, 128).transpose(2, 1, 0)).reshape(128, 248))
    put("cdb", pc(inp["conv_dw_b"][0]))
    put("lng", pc(inp["conv_ln_g"][0]))
    put("lnb", pc(inp["conv_ln_b"][0]))
    put("pw2b", pc(inp["conv_pw2_b"][0]))
    fdw = inp["ffn_dw_w"]
    put("fdw", np.ascontiguousarray(fdw.reshape(2, 3, NCH, 128).transpose(3, 0, 2, 1)).reshape(128, 264))
    put("fdb", np.ascontiguousarray(inp["ffn_dw_b"].reshape(2, NCH, 128).transpose(2, 0, 1)).reshape(128, 88))
    invf = (1.0 / (10000.0 ** (np.arange(0, 64, 2, dtype=np.float32) / 64.0))).astype(f)
    put("invf", np.tile(invf, 4)[:, None])
    put("eps", np.full((128, 1), EPS, f))
    put("halfpi", np.full((128, 1), 0.5 * math.pi, f))
    rows = np.zeros((128, 1024), f)
    rows[:, 0:512] = inp["a_vnorm_g"][0].reshape(1, 512)
    rows[:, 512:1024] = inp["a_spatial_b"][0].reshape(1, 512)
    wsT = np.ascontiguousarray(inp["a_spatial_w"][0].transpose(2, 0, 1))
    p = np.arange(128)
    wmask = (p[None, :] >= p[:, None]).astype(f)
    return {
        "ada_w": np.ascontiguousarray(inp["ada_w"]), "w_in": np.ascontiguousarray(inp["ab_w_in"][0]),
        "w_out": np.ascontiguousarray(inp["ab_w_out"][0]), "up_w": np.ascontiguousarray(inp["ffn_up_w"]),
        "down_w": np.ascontiguousarray(inp["ffn_down_w"]), "pw1": np.ascontiguousarray(inp["conv_pw1_w"][0]),
        "pw2": np.ascontiguousarray(inp["conv_pw2_w"][0]), "wsT": wsT, "cols": cols, "rows": rows,
        "cbf": _const_bf16(), "wmask": wmask,
    }


def _in_maps(inp, n_cores=8):
    inp = {k: np.asarray(v) for k, v in inp.items()}
    shared = _prep_shared(inp)
    maps = []
    for b in range(n_cores):
        m = dict(shared)
        m["xT"] = np.ascontiguousarray(inp["x"][b].T).reshape(8, 128, SEQ)
        m["cT"] = np.ascontiguousarray(inp["c"][b].reshape(8, 128).T)
        m["posr"] = np.ascontiguousarray(np.broadcast_to(inp["positions"][b].astype(np.int32)[None, :], (128, SEQ)))
        maps.append(m)
    return maps


def kernel(**inputs):
    if "nc" not in _CACHE:
        _CACHE["nc"] = build_program()
    nc = _CACHE["nc"]
    maps = _in_maps(inputs)
    res = run_bass_kernel_spmd(nc, maps, core_ids=list(range(8)))
    out = np.stack([r["outT"].reshape(D, SEQ).T for r in res.results], 0)
    return np.ascontiguousarray(out.astype(np.float32))
```

```python
import math
import numpy as np
import ml_dtypes
import concourse.bass as bass
import concourse.mybir as mybir
from concourse.bass_utils import run_bass_kernel_spmd
from contextlib import ExitStack

F32 = mybir.dt.float32
BF16 = mybir.dt.bfloat16
I32 = mybir.dt.int32
AF = mybir.ActivationFunctionType
ALU = mybir.AluOpType
AX = mybir.AxisListType

D = 1024
SEQ = 4096
T = 512
NT = SEQ // T
FF = 2816
NCH = 2 * FF // 128
EPS = 1e-6
NSLAB = 3
HV = 260


class Buf:
    __slots__ = ("name", "w", "r", "const")

    def __init__(self, name, const=False):
        self.name = name
        self.w = None
        self.r = []
        self.const = const


class Op:
    __slots__ = ("eng", "fn", "deps", "idx", "sig", "signo", "dma", "sem_key", "epoch")


class Sched:
    ENGS = ("pe", "act", "dve", "pool", "sp")
    NDMA = {"sp": 16, "pool": 6, "act": 4}

    def __init__(self):
        self.ops = []
        self.epoch = 0
        self.dma_count = {"sp": 0, "pool": 0, "act": 0}
        self.dma_last = {}

    def add(self, eng, fn, reads=(), writes=(), dma=False):
        i = len(self.ops)
        op = Op()
        op.eng, op.fn, op.idx, op.dma, op.sig, op.signo = eng, fn, i, dma, False, 0
        op.epoch = self.epoch
        deps = set()
        for b in reads:
            if b.w is not None:
                deps.add(b.w)
        for b in writes:
            if b.w is not None:
                deps.add(b.w)
            deps.update(b.r)
        for b in reads:
            if not b.const:
                b.r.append(i)
        for b in writes:
            b.w = i
            b.r = []
        deps.discard(i)
        if dma:
            k = self.dma_count[eng]
            self.dma_count[eng] = k + 1
            op.sem_key = ("dma", eng, k % self.NDMA[eng])
            prev = self.dma_last.get(op.sem_key)
            if prev is not None:
                deps.add(prev)
            self.dma_last[op.sem_key] = i
        else:
            op.sem_key = ("eng", eng, self.epoch)
        if eng == "pe" and not dma:
            deps = {d for d in deps if not (self.ops[d].eng == "pe" and not self.ops[d].dma)}
        best = {}
        out = set()
        for d in deps:
            o = self.ops[d]
            if o.dma:
                out.add(d)
            else:
                if o.sem_key not in best or best[o.sem_key] < d:
                    best[o.sem_key] = d
        out.update(best.values())
        op.deps = out
        self.ops.append(op)
        return i

    def emit(self, nc, final_deps=()):
        ops = self.ops
        for op in ops:
            for d in op.deps:
                ops[d].sig = True
        for d in final_deps:
            ops[d].sig = True
        counters = {}
        for op in ops:
            if op.sig:
                c = counters.get(op.sem_key, 0) + (16 if op.dma else 1)
                counters[op.sem_key] = c
                op.signo = c
        with ExitStack() as es:
            sems = {k: es.enter_context(nc.semaphore("s_" + "_".join(map(str, k)))) for k in counters}
            block = es.enter_context(nc.Block())
            per_eng = {e: [op for op in ops if op.eng == e] for e in self.ENGS}

            def body(eng_name, eng):
                seen = {}

                def wait_all(deps):
                    need = {}
                    for d in deps:
                        o = ops[d]
                        if need.get(o.sem_key, 0) < o.signo:
                            need[o.sem_key] = o.signo
                    for k, v in need.items():
                        if seen.get(k, 0) >= v:
                            continue
                        eng.wait_ge(sems[k], v)
                        seen[k] = v

                for op in per_eng[eng_name]:
                    wait_all(op.deps)
                    inst = op.fn(eng)
                    if op.sig:
                        inst.then_inc(sems[op.sem_key], 16 if op.dma else 1)
                if eng_name == "sp":
                    wait_all(final_deps)

            @block.tensor
            def _(e):
                body("pe", e)

            @block.scalar
            def _(e):
                body("act", e)

            @block.vector
            def _(e):
                body("dve", e)

            @block.gpsimd
            def _(e):
                body("pool", e)

            @block.sync
            def _(e):
                body("sp", e)


def _cols_layout():
    names = [("adab", 96), ("gmix", 16), ("gffn", 16), ("gq", 1), ("gk", 1), ("pw1b", 16), ("cdw", 248),
             ("cdb", 8), ("lng", 8), ("lnb", 8), ("pw2b", 8), ("fdw", 264), ("fdb", 88), ("invf", 1),
             ("eps", 1), ("halfpi", 1)]
    off = {}
    o = 0
    for n, w in names:
        off[n] = o
        o += w
    return off, o


COLS, NCOL = _cols_layout()
CB = {"bd64": 0, "rm": 128, "ones1024": 256, "m1a": 384, "m1b": 896, "m4p": 1408, "m4c": 1920,
      "m16_0": 2432, "m16_1": 2944, "m16_2": 3456, "m16_3": 3968, "m16o": 4480, "ident": 4992, "shift": 5120}
NCB = 5248


def _const_bf16():
    c = np.zeros((128, NCB), np.float32)
    p = np.arange(128)
    c[:, 0:128] = ((p[:, None] // 64) == (p[None, :] // 64)) * (1.0 / 64.0)
    rm = np.zeros((128, 128), np.float32)
    for m in range(128):
        if m % 64 < 32:
            rm[m + 32, m] = -1.0
        else:
            rm[m - 32, m] = 1.0
    c[:, 128:256] = rm
    c[:, 256:384] = 1.0 / 1024.0
    k = p[:, None]
    q = np.arange(256)[None, :]
    m1 = ((q - k >= 0) & (q - k <= 128)).astype(np.float32)
    c[:, 384:896] = np.concatenate([m1[:, 128:256], m1[:, 0:256], m1[:, 0:128]], 1)
    c[:, 896:1408] = np.concatenate([m1, m1], 1)
    c[:, 1408:1920] = np.tile(m1[:, 128:256], (1, 4))
    c[:, 1920:2432] = np.tile(m1[:, 0:128], (1, 4))
    j = np.arange(32)[None, :]
    for tt in range(4):
        ma = (k <= 32 * tt + j).astype(np.float32)
        c[:, 2432 + 512 * tt: 2432 + 512 * (tt + 1)] = np.tile(ma, (1, 16))
    mo = ((k >= j) & (k < 32)).astype(np.float32)
    c[:, 4480:4992] = np.tile(mo, (1, 16))
    c[:, 4992:5120] = np.eye(128, dtype=np.float32)
    sh = np.zeros((128, 128), np.float32)
    for kk in range(64):
        sh[kk, kk + 64] = 1.0
    c[:, 5120:5248] = sh
    return c.astype(ml_dtypes.bfloat16)


class KB:
    def __init__(self, nc):
        self.nc = nc
        self.S = Sched()
        self.es = ExitStack()
        self.rot = {}

    def sb(self, name, shape, dt):
        return self.es.enter_context(self.nc.sbuf_tensor("s_" + name, shape, dt))

    def ps(self, name, shape, dt):
        return self.es.enter_context(self.nc.psum_tensor("p_" + name, shape, dt))

    def add(self, eng, fn, r=(), w=(), dma=False):
        return self.S.add(eng, fn, r, w, dma)

    def mm(self, out, lhsT, rhs, start, stop, r, w, skip=False):
        if skip:
            return self.add("pe", lambda e: e.matmul(out, lhsT=lhsT, rhs=rhs, start=start, stop=stop,
                                                     skip_group_check=True), r, w)
        return self.add("pe", lambda e: e.matmul(out, lhsT=lhsT, rhs=rhs, start=start, stop=stop), r, w)

    def act(self, out, in_, func, r, w, scale=1.0, bias=None):
        if bias is None:
            return self.add("act", lambda e: e.activation(out=out, in_=in_, func=func, scale=scale), r, w)
        return self.add("act", lambda e: e.activation(out=out, in_=in_, func=func, scale=scale, bias=bias), r, w)

    def tt(self, out, in0, in1, op, r, w, eng="dve"):
        return self.add(eng, lambda e: e.tensor_tensor(out=out, in0=in0, in1=in1, op=op), r, w)

    def ts(self, out, in0, s1, s2, op0, op1, r, w, eng="dve"):
        if s2 is None:
            return self.add(eng, lambda e: e.tensor_scalar(out=out, in0=in0, scalar1=s1, scalar2=None, op0=op0), r, w)
        return self.add(eng, lambda e: e.tensor_scalar(out=out, in0=in0, scalar1=s1, scalar2=s2, op0=op0, op1=op1), r, w)

    def stt(self, out, in0, scalar, in1, op0, op1, r, w, eng="dve"):
        return self.add(eng, lambda e: e.scalar_tensor_tensor(out=out, in0=in0, scalar=scalar, in1=in1, op0=op0, op1=op1), r, w)

    def copy(self, out, in_, r, w, eng="dve"):
        return self.add(eng, lambda e: e.tensor_copy(out=out, in_=in_), r, w)

    def recip(self, out, in_, r, w):
        return self.add("dve", lambda e: e.reciprocal(out=out, in_=in_), r, w)

    def memset(self, ap, val, w, eng="dve"):
        return self.add(eng, lambda e: e.memset(ap, val), (), w)

    def dma(self, out, in_, r, w, eng="sp"):
        return self.add(eng, lambda e: e.dma_start(out=out, in_=in_), r, w, dma=True)

    def rr(self, key, n):
        i = self.rot.get(key, 0)
        self.rot[key] = i + 1
        return i % n


def build_program(nt=NT, dbg=False):
    nc = bass.Bass("TRN2", target_bir_lowering=False)
    kb = KB(nc)
    S = kb.S

    def din(name, shape, dt):
        return nc.dram_tensor(name, shape, dt, kind="ExternalInput").ap()

    def dscr(name, shape, dt):
        return nc.dram_tensor(name, shape, dt, kind="Internal").ap()

    xT = din("xT", [8, 128, SEQ], F32)
    cT = din("cT", [128, 8], F32)
    posr = din("posr", [128, SEQ], I32)
    ada_w = din("ada_w", [2, D, 6 * D], F32)
    w_in = din("w_in", [D, 2560], F32)
    w_out = din("w_out", [D, D], F32)
    up_w = din("up_w", [2, D, 2 * FF], F32)
    down_w = din("down_w", [2, FF, D], F32)
    pw1 = din("pw1", [D, 2 * D], F32)
    pw2 = din("pw2", [D, D], F32)
    wsT = din("wsT", [128, 4, 128], F32)
    cols_d = din("cols", [128, NCOL], F32)
    rows_d = din("rows", [128, 1024], F32)
    cbf_d = din("cbf", [128, NCB], BF16)
    wmask_d = din("wmask", [128, 128], F32)
    outT = nc.dram_tensor("outT", [8, 128, SEQ], F32, kind="ExternalOutput").ap()
    if dbg:
        dbgT = nc.dram_tensor("dbgT", [4, 8, 128, SEQ], F32, kind="ExternalOutput").ap()

    v_scr = dscr("v_scr", [SEQ, 2, HV], BF16)

    xt = kb.sb("xt", [128, 8, T], F32)
    b_xt = [Buf(f"xt{c}") for c in range(8)]
    hT = kb.sb("hT", [128, 8, T], BF16)
    b_hT = [Buf(f"hT{c}") for c in range(8)]
    slabs = [kb.sb(f"slab{i}", [128, 4096], BF16) for i in range(NSLAB)]
    b_slab = [[Buf(f"slab{i}a"), Buf(f"slab{i}b")] for i in range(NSLAB)]
    kT = kb.sb("kT", [128, 4, SEQ], BF16)
    b_kT = [[Buf(f"kT{c}_{t}") for t in range(NT)] for c in range(4)]
    qT = kb.sb("qT", [128, 4, T], BF16)
    b_qT = [Buf(f"qT{c}") for c in range(4)]
    yT = kb.sb("yT", [128, 8, T], BF16)
    b_yT = [Buf(f"yT{c}") for c in range(8)]
    ua = kb.sb("ua", [128, 4, T], F32)
    b_ua = [Buf(f"ua{c}") for c in range(4)]
    vn = kb.sb("vn", [128, 4, T], BF16)
    b_vn = [Buf(f"vn{c}") for c in range(4)]
    arena = kb.sb("arena", [128, 6656], F32)
    ar16 = arena[:, :].bitcast(BF16)
    V1 = ar16[:, 0:1300].rearrange("p (b c) -> p b c", b=5)
    V4 = ar16[:, 1300:3380].rearrange("p (b c) -> p b c", b=8)
    V16A = ar16[:, 3380:7540].rearrange("p (r c) -> p r c", r=16)
    V16O = ar16[:, 7540:11700].rearrange("p (r c) -> p r c", r=16)
    b_V1 = [Buf("V1")]
    b_V4 = [Buf("V4p"), Buf("V4c")]
    b_V16A = Buf("V16A")
    b_V16O = Buf("V16O")
    gT = ar16[:, 0:22 * T].rearrange("p (k t) -> p k t", k=22)
    b_gT = [Buf(f"gT{k}") for k in range(22)]
    yglu = ar16[:, 0:8 * 542].rearrange("p (c t) -> p c t", c=8)
    b_yglu = [Buf(f"yglu{c}") for c in range(8)]
    ycv = arena[:, 2560:2560 + 4096].rearrange("p (c t) -> p c t", c=8)
    b_ycv = [Buf(f"ycv{c}") for c in range(8)]
    arena_bufs = b_V1 + b_V4 + [b_V16A, b_V16O] + b_gT + b_yglu + b_ycv

    cbf = kb.sb("cbf", [128, NCB], BF16)
    b_cbf = Buf("cbf", const=True)
    cols = kb.sb("cols", [128, NCOL], F32)
    b_cols = Buf("cols", const=True)
    rows = kb.sb("rows", [128, 1024], F32)
    b_rows = Buf("rows", const=True)
    wsb = kb.sb("wsb", [128, 4, 128], BF16)
    b_wsb = Buf("wsb", const=True)
    onesf = kb.sb("onesf", [128, 128], F32)
    b_onesf = Buf("onesf", const=True)
    modc = kb.sb("modc", [128, 96], F32)
    b_modc = [Buf("modc0", const=True), Buf("modc1", const=True)]
    dcol = kb.sb("dcol", [128, 48], F32)
    b_dcol = [Buf("dcol0", const=True), Buf("dcol1", const=True)]
    cact = kb.sb("cact", [128, 8], BF16)
    b_cact = Buf("cact", const=True)
    NTF = 6
    tf = [kb.sb(f"tf{i}", [128, T], F32) for i in range(NTF)]
    b_tf = [Buf(f"tf{i}") for i in range(NTF)]
    NTB = 6
    tb = [kb.sb(f"tb{i}", [128, T], BF16) for i in range(NTB)]
    b_tb = [Buf(f"tb{i}") for i in range(NTB)]
    zs = [kb.sb(f"zs{i}", [128, T + 2], F32) for i in range(3)]
    b_zs = [Buf(f"zs{i}") for i in range(3)]
    b_zh = [Buf(f"zh{i}") for i in range(3)]
    acc = [kb.sb(f"acc{i}", [128, T], F32) for i in range(3)]
    b_acc = [Buf(f"acc{i}") for i in range(3)]
    sa = [kb.sb(f"sa{i}", [128, T], F32) for i in range(2)]
    b_sa = [Buf(f"sa{i}") for i in range(2)]
    halo = kb.sb("halo", [128, 2 * NCH * 2], F32)
    b_halo = [[Buf(f"halo{l}_{c}") for c in range(NCH)] for l in range(2)]
    vst = kb.sb("vst", [128, 4, 2 * HV], BF16)
    b_vst = [Buf(f"vst{i}") for i in range(4)]
    cst = kb.sb("cst", [128, 2, T], F32)
    b_cst = Buf("cst")
    posf = kb.sb("posf", [128, T], I32)
    b_posf = Buf("posf")
    yhalo = kb.sb("yhalo", [128, 8, 30], BF16)
    b_yhalo = [Buf(f"yhalo{c}") for c in range(8)]
    small = kb.sb("small", [128, 96], F32)
    b_small = Buf("small")
    b_s1 = [Buf(f"s1_{i}") for i in range(4)]
    b_s2 = [Buf(f"s2_{i}") for i in range(4)]
    b_sm, b_sv, b_snb = Buf("sm"), Buf("sv"), Buf("snb")
    rstd_t = kb.sb("rstd_t", [128, T], F32)
    b_rstd_t = Buf("rstd_t")
    ln_mu = kb.sb("ln_mu", [128, T], F32)
    b_ln_mu = Buf("ln_mu")
    rd, b_rd = ln_mu, b_ln_mu

    pbank = [kb.ps(f"pb{i}", [128, T], F32) for i in range(8)]
    b_pb = [Buf(f"pb{i}") for i in range(8)]
    PCL = {"A": [0, 1, 2], "B": [3, 4, 5], "C": [6], "D": [7], "O": [6, 7]}

    def psum(cl):
        lst = PCL[cl]
        i = lst[kb.rr("ps" + cl, len(lst))]
        return pbank[i], b_pb[i]

    def tmpf():
        i = kb.rr("tf", NTF)
        return tf[i], b_tf[i]

    def tmpb():
        i = kb.rr("tb", NTB)
        return tb[i], b_tb[i]

    def col(name, i=0, n=1):
        o = COLS[name] + i
        return cols[:, o:o + n]

    def cb(name, w=512):
        o = CB[name]
        return cbf[:, o:o + w]

    def load_x(t):
        kb.dma(xt[:, :, :], xT[:, :, t * T:(t + 1) * T].rearrange("c p t -> p c t"), [], b_xt)

    load_x(0)
    kb.dma(cbf[:, :], cbf_d, [], [b_cbf])
    kb.dma(cols[:, :], cols_d, [], [b_cols])
    kb.dma(rows[:, :], rows_d, [], [b_rows])
    ti, b_ti = tmpf()
    kb.dma(ti[:, 0:128], wmask_d, [], [b_ti])
    t2, b_t2 = tmpf()
    kb.dma(t2[:, :], wsT.rearrange("p g t -> p (g t)"), [], [b_t2])
    for g in range(4):
        kb.tt(wsb[:, g, :], t2[:, g * 128:(g + 1) * 128], ti[:, 0:128], ALU.mult, [b_t2, b_ti], [b_wsb])
    kb.memset(onesf[:, :], 1.0, [b_onesf])
    kb.memset(halo[:, :], 0.0, [b for l in b_halo for b in l])
    kb.memset(vst[:, :, :], 1.0, b_vst)
    b_w = {}

    t3, b_t3 = tmpf()
    kb.dma(t3[:, 0:8], cT, [], [b_t3])
    kb.act(cact[:, :], t3[:, 0:8], AF.Silu, [b_t3], [b_cact])
    slab_state = {"n": 0}

    def next_slab():
        i = slab_state["n"] % NSLAB
        slab_state["n"] += 1
        return slabs[i], b_slab[i]

    modp, b_modp = psum("C")

    def ada_slab(l, n):
        sl, b_sl = next_slab()
        kb.dma(sl[:, :].rearrange("p (k n) -> p k n", k=8),
               ada_w[l].rearrange("(k p) n -> p k n", p=128)[:, :, n * 512:(n + 1) * 512], [], b_sl, eng="pool")
        for jj in range(4):
            j = 4 * n + jj
            for kc in range(8):
                kb.mm(modp[:, l * 48 + j: l * 48 + j + 1],
                      sl[:, kc * 512 + jj * 128: kc * 512 + (jj + 1) * 128], cact[:, kc:kc + 1],
                      kc == 0, kc == 7, b_sl + [b_cact], [b_modp])

    def ada_finish(l):
        kb.tt(modc[:, l * 48:(l + 1) * 48], modp[:, l * 48:(l + 1) * 48], col("adab", l * 48, 48), ALU.add,
              [b_modp, b_cols], [b_modc[l]])
        for sub in range(2):
            o = (l * 2 + sub) * 8
            gname = "gmix" if sub == 0 else "gffn"
            kb.stt(dcol[:, o:o + 8], modc[:, l * 48 + (3 * sub + 1) * 8: l * 48 + (3 * sub + 1) * 8 + 8], 1.0,
                   col(gname, l * 8, 8), ALU.add, ALU.mult, [b_modc[l], b_cols], [b_dcol[l]])
        if l == 1:
            kb.tt(dcol[:, 32:40], col("pw2b", 0, 8), modc[:, 48 + 16: 48 + 24], ALU.mult, [b_modc[1], b_cols],
                  [b_dcol[1]])

    for l_ in range(2):
        for n in range(12):
            ada_slab(l_, n)
        ada_finish(l_)

    def gmcol(l, sub, c):
        o = (l * 2 + sub) * 8 + c
        return dcol[:, o:o + 1]

    def shcol(l, sub, c):
        o = l * 48 + (3 * sub) * 8 + c
        return modc[:, o:o + 1]

    def gatecol(l, sub, c):
        o = l * 48 + (3 * sub + 2) * 8 + c
        return modc[:, o:o + 1]

    wscr = dscr("wscr", [64, 128, 4096], BF16)
    slab_ids = {}
    b_wscr = {}

    def get_w(key, parts, build=None):
        sl, b_sl = next_slab()
        used = max([k * ncols for (k, ncols, c0, w, src, bi) in parts] + ([31 * 128] if build is not None else []))
        if key not in slab_ids:
            sid = len(slab_ids)
            slab_ids[key] = sid
            if build is not None:
                build(sl, b_sl)
            for (k, ncols, c0, w, src, bi) in parts:
                view = sl[:, 0:k * ncols].rearrange("p (k n) -> p k n", k=k)[:, :, c0:c0 + w]
                kb.dma(view, src, [], b_sl if bi is None else [b_sl[bi]], eng="pool")
            b = Buf(f"wscr{sid}")
            b_wscr[sid] = b
            kb.dma(wscr[sid][:, 0:used], sl[:, 0:used], b_sl, [b], eng="sp")
        else:
            sid = slab_ids[key]
            kb.dma(sl[:, 0:used], wscr[sid][:, 0:used], [b_wscr[sid]], b_sl, eng="sp")
        return sl, b_sl

    def w_kpn(w2d):
        return w2d.rearrange("(k p) n -> p k n", p=128)

    def norm_mod(l, sub):
        ssb, b_ssb = psum("C")
        for c in range(8):
            sq, b_sq = tmpb()
            kb.act(sq[:, :], xt[:, c, :], AF.Square, [b_xt[c]], [b_sq])
            kb.mm(ssb[:, :], cb("ones1024", 128), sq[:, :], c == 0, c == 7, [b_sq, b_cbf], [b_ssb])
        sd, b_sd = tmpf()
        kb.act(sd[:, :], ssb[:, :], AF.Ln, [b_ssb, b_cols], [b_sd], bias=col("eps"))
        rs, b_rs = rstd_t, b_rstd_t
        kb.act(rs[:, :], sd[:, :], AF.Exp, [b_sd], [b_rs], scale=-0.5)
        for c in range(8):
            t, b_t = tmpf()
            kb.stt(t[:, :], xt[:, c, :], gmcol(l, sub, c), rs[:, :], ALU.mult, ALU.mult,
                   [b_xt[c], b_rs, b_dcol[l]], [b_t])
            kb.act(hT[:, c, :], t[:, :], AF.Identity, [b_t, b_modc[l]], [b_hT[c]], bias=shcol(l, sub, c))

    def proj_fm(sl, b_sl, ncols, col0, src, b_src, nk=8):
        bank, b_bank = psum("A")
        for kc in range(nk):
            kb.mm(bank[:, :], sl[:, kc * ncols + col0: kc * ncols + col0 + 128], src[:, kc, :],
                  kc == 0, kc == nk - 1, b_sl + [b_src[kc]], [b_bank])
        return bank, b_bank

    def proj_tm(sl, b_sl, tc, src, b_src):
        bank, b_bank = psum("A")
        for kc in range(8):
            kb.mm(bank[:, :], src[:, kc, tc * 128:(tc + 1) * 128], sl[:, kc * 512:(kc + 1) * 512],
                  kc == 0, kc == 7, b_sl + [b_src[kc]], [b_bank])
        return bank, b_bank

    def arena_barrier():
        kb.memset(small[:, 90:91], 0.0, [b_small] + arena_bufs)

    def dbg_dump(stage, t):
        if dbg:
            for c in range(8):
                kb.dma(dbgT[stage, c, :, t * T:(t + 1) * T], xt[:, c, :], [b_xt[c]], [])

    def qk_stages(sl, b_sl, fc, gname, dst, b_dst):
        st = {}

        def sA():
            st["qp"] = proj_fm(sl, b_sl, 512, fc * 128, hT, b_hT)
            qp, b_qp = st["qp"]
            st["sq"] = tmpb()
            sq, b_sq = st["sq"]
            kb.act(sq[:, :], qp[:, :], AF.Square, [b_qp], [b_sq])

        def sB():
            qp, b_qp = st["qp"]
            sq, b_sq = st["sq"]
            ssb, b_ssb = psum("C")
            kb.mm(ssb[:, :], cb("bd64", 128), sq[:, :], True, True, [b_sq, b_cbf], [b_ssb])
            sd, b_sd = tmpf()
            kb.act(sd[:, :], ssb[:, :], AF.Ln, [b_ssb, b_cols], [b_sd], bias=col("eps"))
            rs, b_rs = tmpf()
            kb.act(rs[:, :], sd[:, :], AF.Exp, [b_sd], [b_rs], scale=-0.5)
            st["q1"] = tmpb()
            q1, b_q1 = st["q1"]
            kb.stt(q1[:, :], qp[:, :], col(gname), rs[:, :], ALU.mult, ALU.mult, [b_qp, b_rs, b_cols], [b_q1])

        def sC():
            q1, b_q1 = st["q1"]
            rot, b_rot = psum("D")
            kb.mm(rot[:, :], cb("rm", 128), q1[:, :], True, True, [b_q1, b_cbf], [b_rot])
            t1, b_t1 = tmpf()
            kb.tt(t1[:, :], q1[:, :], cst[:, 0, :], ALU.mult, [b_q1, b_cst], [b_t1])
            t2_, b_t2_ = tmpf()
            kb.tt(t2_[:, :], rot[:, :], cst[:, 1, :], ALU.mult, [b_rot, b_cst], [b_t2_])
            kb.tt(dst, t1[:, :], t2_[:, :], ALU.add, [b_t1, b_t2_], [b_dst])

        return [sA, sB, sC]

    def l0_mixer(t):
        arena_barrier()
        norm_mod(0, 0)
        kb.dma(posf[:, :], posr[:, t * T:(t + 1) * T], [], [b_posf])
        ang, b_ang = tmpf()
        kb.copy(ang[:, :], posf[:, :], [b_posf], [b_ang])
        kb.ts(ang[:, :], ang[:, :], col("invf"), None, ALU.mult, None, [b_ang, b_cols], [b_ang])
        u_, b_u = tmpf()
        kb.ts(u_[:, :], ang[:, :], 1.0 / (2.0 * math.pi), None, ALU.mult, None, [b_ang], [b_u])
        kb.copy(posf[:, :], u_[:, :], [b_u], [b_posf])
        kb.copy(u_[:, :], posf[:, :], [b_posf], [b_u])
        r_, b_r = tmpf()
        kb.stt(r_[:, :], u_[:, :], -2.0 * math.pi, ang[:, :], ALU.mult, ALU.add, [b_u, b_ang], [b_r])
        sh_, b_sh = tmpf()
        kb.act(sh_[:, :], r_[:, :], AF.Sin, [b_r], [b_sh], scale=0.5)
        ch_, b_ch = tmpf()
        kb.act(ch_[:, :], r_[:, :], AF.Sin, [b_r, b_cols], [b_ch], scale=-0.5, bias=col("halfpi"))
        kb.stt(cst[:, 1, :], sh_[:, :], 2.0, ch_[:, :], ALU.mult, ALU.mult, [b_sh, b_ch], [b_cst])
        kb.tt(sh_[:, :], sh_[:, :], sh_[:, :], ALU.mult, [b_sh], [b_sh])
        kb.ts(cst[:, 0, :], sh_[:, :], -2.0, 1.0, ALU.mult, ALU.add, [b_sh], [b_cst])

        wk = w_kpn(w_in)
        sl, b_sl = get_w("win0", [(8, 512, 0, 512, wk[:, :, 0:512], None)])
        for fc in range(4):
            bank, b_bank = proj_fm(sl, b_sl, 512, fc * 128, hT, b_hT)
            kb.act(ua[:, fc, :], bank[:, :], AF.Gelu, [b_bank], [b_ua[fc]])
        sl, b_sl = get_w("win512", [(8, 512, 0, 512, wk[:, :, 512:1024], None)])
        S1, S2, M_, V_ = small[:, 0:16], small[:, 16:32], small[:, 32:48], small[:, 48:64]
        kb.memset(S2, 0.0, b_s2)
        vgs = []
        for tc in range(4):
            bank, b_bank = proj_tm(sl, b_sl, tc, hT, b_hT)
            vg, b_vg = tmpf()
            kb.act(vg[:, :], bank[:, :], AF.Gelu, [b_bank], [b_vg])
            kb.add("dve", lambda e, tc=tc, vg=vg: e.tensor_reduce(
                out=small[:, tc * 4:(tc + 1) * 4], in_=vg[:, :].rearrange("p (g d) -> p g d", g=4), axis=AX.X,
                op=ALU.add), [b_vg], [b_s1[tc]])
            for g in range(4):
                junk, b_junk = tmpb()
                kb.add("act", lambda e, tc=tc, g=g, vg=vg, junk=junk: e.activation(
                    out=junk[:, 0:128], in_=vg[:, g * 128:(g + 1) * 128], func=AF.Square,
                    accum_out=small[:, 16 + tc * 4 + g: 17 + tc * 4 + g]), [b_vg], [b_junk, b_s2[tc]])
            vgs.append((vg, b_vg))
        kb.ts(M_, S1, 1.0 / 128.0, None, ALU.mult, None, b_s1, [b_sm])
        kb.tt(V_, M_, M_, ALU.mult, [b_sm], [b_sv])
        kb.stt(V_, S2, 1.0 / 128.0, V_, ALU.mult, ALU.subtract, b_s2 + [b_sv], [b_sv])
        kb.act(V_, V_, AF.Ln, [b_sv, b_cols], [b_sv], bias=col("eps"))
        kb.act(V_, V_, AF.Exp, [b_sv], [b_sv], scale=-0.5)
        kb.stt(S1, M_, -1.0, V_, ALU.mult, ALU.mult, [b_sm, b_sv] + b_s1, [b_snb])
        for tc in range(4):
            vg, b_vg = vgs[tc]
            for g in range(4):
                o = tc * 4 + g
                kb.act(vg[:, g * 128:(g + 1) * 128], vg[:, g * 128:(g + 1) * 128], AF.Identity,
                       [b_vg, b_sv, b_snb], [b_vg], scale=small[:, 48 + o:49 + o], bias=small[:, o:o + 1])
            kb.tt(vn[:, tc, :], vg[:, :], rows[:, 0:512], ALU.mult, [b_vg, b_rows], [b_vn[tc]])
        slq, b_slq = get_w("win1024", [(8, 512, 0, 512, wk[:, :, 1024:1536], None)])
        slk, b_slk = get_w("win1536", [(8, 512, 0, 512, wk[:, :, 1536:2048], None)])
        stg = [qk_stages(slq, b_slq, fc, "gq", qT[:, fc, :], b_qT[fc]) for fc in range(4)]
        stg += [qk_stages(slk, b_slk, fc, "gk", kT[:, fc, t * T:(t + 1) * T], b_kT[fc][t]) for fc in range(4)]
        n_ = len(stg)
        for i in range(n_ + 2):
            if i < n_:
                stg[i][0]()
            if 0 <= i - 1 < n_:
                stg[i - 1][1]()
            if 0 <= i - 2 < n_:
                stg[i - 2][2]()
        sl, b_sl = get_w("win2048", [(8, 512, 0, 512, wk[:, :, 2048:2560], None)])
        b_vrow = []
        for tc in range(4):
            bank, b_bank = proj_tm(sl, b_sl, tc, hT, b_hT)
            kb.act(vst[:, tc, :].rearrange("p (h e) -> p h e", e=65)[:, :, 0:64],
                   bank[:, :].rearrange("p (h e) -> p h e", e=64), AF.Identity, [b_bank], [b_vst[tc]])
            b = Buf(f"vrow{t}_{tc}")
            kb.dma(v_scr[t * T + tc * 128: t * T + (tc + 1) * 128].rearrange("p a c -> p (a c)"), vst[:, tc, :],
                   [b_vst[tc]], [b])
            b_vrow.append(b)
        b_vtile[t] = b_vrow

        ks = max(0, 32 * (t - 3))
        nk = 32 * (t + 1) - ks

        def head_tasks(hl, h):
            c = h // 2
            pb = 64 * (h % 2)
            vc = slice(hl * 65, hl * 65 + 65)
            O, b_O = psum("O")
            first = [True]
            kTc = kT[pb:pb + 64, c, :]
            qTc = qT[pb:pb + 64, c, :]
            rdq = [b_qT[c]]
            tasks = []

            def pv(ocols, lhsT, rhs, rds):
                if isinstance(ocols, tuple):
                    oap = O[0:65, :].rearrange("p (i s) -> p i s", s=ocols[1])[:, :, ocols[0]]
                else:
                    oap = O[0:65, ocols]
                kb.mm(oap, lhsT, rhs, first[0], False, rds, [b_O], skip=True)
                first[0] = False

            def add_task(qk_list, np_, c0, mask, pv_list):
                st = {}

                def qk():
                    Sb, b_Sb = psum("B")
                    st["S"] = (Sb, b_Sb)
                    for (oc, lh, rh, rds) in qk_list:
                        kb.mm(Sb[0:np_, oc], lh, rh, True, True, rds, [b_Sb])

                def post():
                    Sb, b_Sb = st["S"]
                    P, b_P = tmpb()
                    kb.act(P[0:np_, c0:512], Sb[0:np_, c0:512], AF.Exp, [b_Sb], [b_P], scale=0.125)
                    kb.tt(P[0:np_, c0:512], P[0:np_, c0:512], mask[0:np_, c0:512], ALU.mult, [b_P, b_cbf], [b_P])
                    for (ocols, lh, pc, rds) in pv_list:
                        pv(ocols, lh, P[0:np_, pc], [b_P] + rds)

                tasks.append([qk, post])

            units1 = [(4 * t - 1, 0, 0, 128), (4 * t, 128, 0, 256), (4 * t + 3, 384, 384, 128)]
            units2 = [(4 * t + 1, 0, 128, 256), (4 * t + 2, 256, 256, 256)]
            for ui, units in enumerate((units1, units2)):
                units = [u for u in units if u[0] >= 0]
                lo = min(u[1] for u in units)
                qk_list = [(slice(bc, bc + w), kTc[:, kbk * 128:(kbk + 1) * 128], qTc[:, q0:q0 + w],
                            [b_kT[c][kbk // 4]] + rdq) for (kbk, bc, q0, w) in units]
                pv_list = [(slice(q0, q0 + w), V1[:, kbk - (4 * t - 1), vc], slice(bc, bc + w), [b_V1[0]])
                           for (kbk, bc, q0, w) in units]
                add_task(qk_list, 128, lo, cb("m1a" if ui == 0 else "m1b"), pv_list)
            for which in (0, 1):
                tk = t - 1 + which
                if tk < 0:
                    continue
                qk_list = [(slice(r * 128, (r + 1) * 128),
                            kTc[:, tk * T:(tk + 1) * T].rearrange("p (i s) -> p i s", s=4)[:, :, r],
                            qTc.rearrange("p (i s) -> p i s", s=4)[:, :, r], [b_kT[c][tk]] + rdq) for r in range(4)]
                pv_list = [((r, 4), V4[:, 4 * which + r, vc], slice(r * 128, (r + 1) * 128), [b_V4[which]])
                           for r in range(4)]
                add_task(qk_list, 128, 0, cb("m4p" if which == 0 else "m4c"), pv_list)
            rdk = [b_kT[c][tt_] for tt_ in range(max(0, t - 3), t + 1)]
            qk_list = [(slice(r * 32, (r + 1) * 32),
                        kTc[:, 16 * ks:16 * (ks + nk)].rearrange("p (i s) -> p i s", s=16)[:, :, r],
                        qTc.rearrange("p (i s) -> p i s", s=16)[:, :, r], rdk + rdq) for r in range(16)]
            pv_list = [((r, 16), V16A[0:nk, r, vc], slice(r * 32, (r + 1) * 32), [b_V16A]) for r in range(16)]
            add_task(qk_list, nk, 0, cb(f"m16_{min(t, 3)}"), pv_list)
            if t >= 4:
                qk_list = [(slice(r * 32, (r + 1) * 32),
                            kTc[:, T * (t - 4):T * (t - 3)].rearrange("p (i s) -> p i s", s=16)[:, :, r],
                            qTc.rearrange("p (i s) -> p i s", s=16)[:, :, r], [b_kT[c][t - 4]] + rdq)
                           for r in range(16)]
                pv_list = [((r, 16), V16O[0:32, r, vc], slice(r * 32, (r + 1) * 32), [b_V16O]) for r in range(16)]
                add_task(qk_list, 32, 0, cb("m16o"), pv_list)

            def finish():
                kb.act(rd[64:65, :], O[64:65, :], AF.Ln, [b_O], [b_rd])
                kb.act(rd[64:65, :], rd[64:65, :], AF.Exp, [b_rd], [b_rd], scale=-1.0)
                BC, b_BC = psum("A")
                kb.mm(BC[0:64, :], onesf[64:65, 0:64], rd[64:65, :], True, True, [b_rd, b_onesf], [b_BC])
                on, b_on = tmpf()
                kb.act(on[0:64, :], O[0:64, :], AF.Identity, [b_O], [b_on])
                if pb == 0:
                    kb.tt(yT[0:64, 4 + c, :], on[0:64, :], BC[0:64, :], ALU.mult, [b_on, b_BC], [b_yT[4 + c]])
                else:
                    yb, b_yb = tmpb()
                    kb.tt(yb[0:64, :], on[0:64, :], BC[0:64, :], ALU.mult, [b_on, b_BC], [b_yb])
                    SH, b_SH = psum("A")
                    kb.mm(SH[:, :], cbf[0:64, CB["shift"]:CB["shift"] + 128], yb[0:64, :], True, True,
                          [b_yb, b_cbf], [b_SH])
                    kb.copy(yT[64:128, 4 + c, :], SH[64:128, :], [b_SH], [b_yT[4 + c]])

            tasks[-1].append(finish)
            return tasks

        for hh in range(2):
            rd2 = b_vtile[t] + (b_vtile[t - 1] if t > 0 else [])
            b0 = 1 if t == 0 else 0
            kb.dma(V1[:, b0:5, :], v_scr[(4 * t - 1 + b0) * 128:(t + 1) * T, hh, :].rearrange("(b p) c -> p b c", p=128),
                   rd2, b_V1)
            if t > 0:
                kb.dma(V4[:, 0:4, :], v_scr[(t - 1) * T:t * T, hh, :].rearrange("(p r) c -> p r c", r=4),
                       b_vtile[t - 1], [b_V4[0]])
            kb.dma(V4[:, 4:8, :], v_scr[t * T:(t + 1) * T, hh, :].rearrange("(p r) c -> p r c", r=4),
                   b_vtile[t], [b_V4[1]])
            rd_v = [b for tt_ in range(max(0, t - 3), t + 1) for b in b_vtile[tt_]]
            kb.dma(V16A[0:nk, :, :], v_scr[16 * ks:16 * (ks + nk), hh, :].rearrange("(p r) c -> p r c", r=16),
                   rd_v, [b_V16A])
            if t >= 4:
                kb.dma(V16O[0:32, :, :], v_scr[T * (t - 4):T * (t - 3), hh, :].rearrange("(p r) c -> p r c", r=16),
                       b_vtile[t - 4], [b_V16O])
            tasks = []
            for hl in range(4):
                tasks += head_tasks(hl, hh * 4 + hl)
            LOOK = 2
            for i in range(min(LOOK, len(tasks))):
                tasks[i][0]()
            DEFER = 2
            for i in range(len(tasks) + DEFER):
                if i + LOOK < len(tasks):
                    tasks[i + LOOK][0]()
                if i < len(tasks):
                    tasks[i][1]()
                if 0 <= i - DEFER < len(tasks) and len(tasks[i - DEFER]) > 2:
                    tasks[i - DEFER][2]()
        for g in range(4):
            bank, b_bank = psum("A")
            for tc in range(4):
                kb.mm(bank[:, tc * 128:(tc + 1) * 128], vn[:, tc, g * 128:(g + 1) * 128], wsb[:, g, :], True, True,
                      [b_vn[tc], b_wsb], [b_bank])
            f, b_f = tmpf()
            for tc in range(4):
                kb.tt(f[:, tc * 128:(tc + 1) * 128], bank[:, tc * 128:(tc + 1) * 128],
                      rows[:, 512 + g * 128: 512 + (g + 1) * 128], ALU.add, [b_bank, b_rows], [b_f])
            kb.tt(yT[:, g, :], f[:, :], ua[:, g, :], ALU.mult, [b_f, b_ua[g]], [b_yT[g]])
        wo = w_kpn(w_out)
        for half in range(2):
            sl, b_sl = get_w(f"wout{half}", [(8, 512, 0, 512, wo[:, :, half * 512:(half + 1) * 512], None)])
            for fcl in range(4):
                fc = half * 4 + fcl
                bank, b_bank = proj_fm(sl, b_sl, 512, fcl * 128, yT, b_yT)
                kb.stt(xt[:, fc, :], bank[:, :], gatecol(0, 0, fc), xt[:, fc, :], ALU.mult, ALU.add,
                       [b_bank, b_modc[0], b_xt[fc]], [b_xt[fc]])

    b_vtile = {}

    def ffn(l, t):
        arena_barrier()
        norm_mod(l, 1)
        wk = w_kpn(up_w[l])
        wname = f"up{l}"
        pend = []
        sas = {}
        for pr in range(11):
            sl, b_sl = get_w(f"{wname}_{pr}", [(8, 512, 0, 256, wk[:, :, pr * 256:(pr + 1) * 256], 0),
                                               (8, 512, 256, 256, wk[:, :, FF + pr * 256: FF + (pr + 1) * 256], 1)])
            chans = [2 * pr, 2 * pr + 1, 22 + 2 * pr, 22 + 2 * pr + 1]
            for ci, ch in enumerate(chans):
                bank, b_bank = proj_fm(sl, [b_sl[ci // 2]], 512, ci * 128, hT, b_hT)
                zi = kb.rr("zs", 3)
                z, b_z = zs[zi], b_zs[zi]
                ai = kb.rr("acc", 3)
                a, b_a = acc[ai], b_acc[ai]
                hcol = halo[:, (l * NCH + ch) * 2:(l * NCH + ch) * 2 + 2]
                kb.act(z[:, 2:T + 2], bank[:, :], AF.Identity, [b_bank], [b_z])
                kb.act(z[:, 0:2], hcol, AF.Identity, [b_halo[l][ch]], [b_zh[zi]])
                wo_ = COLS["fdw"] + (l * NCH + ch) * 3
                kb.act(a[:, :], bank[:, :], AF.Identity, [b_bank, b_cols], [b_a], scale=cols[:, wo_ + 2:wo_ + 3],
                       bias=col("fdb", l * NCH + ch))
                kb.act(hcol, bank[:, T - 2:T], AF.Identity, [b_bank], [b_halo[l][ch]])
                kb.stt(a[:, :], z[:, 1:T + 1], cols[:, wo_ + 1:wo_ + 2], a[:, :], ALU.mult, ALU.add,
                       [b_z, b_zh[zi], b_a, b_cols], [b_a])
                kb.stt(a[:, :], z[:, 0:T], cols[:, wo_:wo_ + 1], a[:, :], ALU.mult, ALU.add,
                       [b_z, b_zh[zi], b_a, b_cols], [b_a])
                if pend:
                    pend.pop(0)()

                def tail(ci=ci, a=a, b_a=b_a, pr=pr):
                    if ci < 2:
                        si = kb.rr("sa", 2)
                        kb.act(sa[si][:, :], a[:, :], AF.Silu, [b_a], [b_sa[si]])
                        sas[ci] = si
                    else:
                        si = sas[ci - 2]
                        k_ = 2 * pr + (ci - 2)
                        kb.tt(gT[:, k_, :], sa[si][:, :], a[:, :], ALU.mult, [b_sa[si], b_a], [b_gT[k_]])

                pend.append(tail)
        while pend:
            pend.pop(0)()
        wd = down_w[l].rearrange("(k p) n -> p k n", p=128)
        dname = f"down{l}"
        for cp in range(4):
            banks = [psum("A") for _ in range(2)]
            for kh in range(2):
                sl, b_sl = get_w(f"{dname}_{cp}_{kh}", [(11, 256, 0, 256, wd[:, kh * 11:(kh + 1) * 11, cp * 256:(cp + 1) * 256], None)])
                for fl in range(2):
                    bank, b_bank = banks[fl]
                    for kk in range(11):
                        k_ = kh * 11 + kk
                        kb.mm(bank[:, :], sl[:, kk * 256 + fl * 128: kk * 256 + (fl + 1) * 128], gT[:, k_, :],
                              k_ == 0, k_ == 21, b_sl + [b_gT[k_]], [b_bank])
            for fl in range(2):
                fc = cp * 2 + fl
                bank, b_bank = banks[fl]
                kb.stt(xt[:, fc, :], bank[:, :], gatecol(l, 1, fc), xt[:, fc, :], ALU.mult, ALU.add,
                       [b_bank, b_modc[l], b_xt[fc]], [b_xt[fc]])

    def l1_mixer(t):
        arena_barrier()
        norm_mod(1, 0)
        wk = w_kpn(pw1)
        for c in range(8):
            if t == 0:
                kb.memset(yglu[:, c, 0:30], 0.0, [b_yglu[c]])
            else:
                kb.copy(yglu[:, c, 0:30], yhalo[:, c, :], [b_yhalo[c]], [b_yglu[c]])
        for pr in range(4):
            sl, b_sl = get_w(f"pw1_{pr}", [(8, 512, 0, 256, wk[:, :, pr * 256:(pr + 1) * 256], 0),
                                           (8, 512, 256, 256, wk[:, :, D + pr * 256: D + (pr + 1) * 256], 1)])
            sgs = []
            for ci in (2, 3, 0, 1):
                bank, b_bank = proj_fm(sl, [b_sl[ci // 2]], 512, ci * 128, hT, b_hT)
                if ci >= 2:
                    ch = 8 + 2 * pr + (ci - 2)
                    sg, b_sg = tmpf()
                    kb.act(sg[:, :], bank[:, :], AF.Sigmoid, [b_bank, b_cols], [b_sg], bias=col("pw1b", ch))
                    sgs.append((sg, b_sg))
                else:
                    ch = 2 * pr + ci
                    sg, b_sg = sgs[ci]
                    kb.stt(yglu[:, ch, 30:30 + T], bank[:, :], col("pw1b", ch), sg[:, :], ALU.add, ALU.mult,
                           [b_bank, b_sg, b_cols], [b_yglu[ch]])
        mean, b_mean = psum("C")
        ex2, b_ex2 = psum("D")
        for c in range(8):
            def build_dg(sl_, b_sl_, c=c):
                kb.tt(sl_[:, 0:31 * 128].rearrange("p (j m) -> p j m", j=31),
                      cb("ident", 128).unsqueeze(1).to_broadcast([128, 31, 128]),
                      col("cdw", c * 31, 31).unsqueeze(2).to_broadcast([128, 31, 128]), ALU.mult,
                      [b_cbf, b_cols], b_sl_)

            sl, b_sl = get_w(f"dg{c}", [], build=build_dg)
            bank, b_bank = psum("A")
            for j in range(31):
                kb.mm(bank[:, :], sl[:, j * 128:(j + 1) * 128], yglu[:, c, j:j + T], j == 0, j == 30,
                      b_sl + [b_yglu[c]], [b_bank])
            kb.copy(yhalo[:, c, :], yglu[:, c, T:T + 30], [b_yglu[c]], [b_yhalo[c]])
            kb.act(ycv[:, c, :], bank[:, :], AF.Identity, [b_bank, b_cols], [b_ycv[c]], bias=col("cdb", c))
            ycb, b_ycb = tmpb()
            kb.act(ycb[:, :], bank[:, :], AF.Identity, [b_bank, b_cols], [b_ycb], bias=col("cdb", c))
            sq, b_sq = tmpb()
            kb.act(sq[:, :], bank[:, :], AF.Square, [b_bank, b_cols], [b_sq], bias=col("cdb", c))
            kb.mm(mean[:, :], cb("ones1024", 128), ycb[:, :], c == 0, c == 7, [b_ycb, b_cbf], [b_mean])
            kb.mm(ex2[:, :], cb("ones1024", 128), sq[:, :], c == 0, c == 7, [b_sq, b_cbf], [b_ex2])
        mu, b_mu = ln_mu, b_ln_mu
        kb.act(mu[:, :], mean[:, :], AF.Identity, [b_mean], [b_mu])
        msq, b_msq = tmpf()
        kb.tt(msq[:, :], mu[:, :], mu[:, :], ALU.mult, [b_mu], [b_msq])
        var, b_var = rstd_t, b_rstd_t
        kb.tt(var[:, :], ex2[:, :], msq[:, :], ALU.subtract, [b_ex2, b_msq], [b_var])
        kb.act(var[:, :], var[:, :], AF.Ln, [b_var, b_cols], [b_var], bias=col("eps"))
        kb.act(var[:, :], var[:, :], AF.Exp, [b_var], [b_var], scale=-0.5)
        for c in range(8):
            d_, b_d = tmpf()
            kb.tt(d_[:, :], ycv[:, c, :], mu[:, :], ALU.subtract, [b_ycv[c], b_mu], [b_d])
            kb.tt(d_[:, :], d_[:, :], var[:, :], ALU.mult, [b_d, b_var], [b_d])
            kb.act(hT[:, c, :], d_[:, :], AF.Silu, [b_d, b_cols], [b_hT[c]], scale=col("lng", c), bias=col("lnb", c))
        wp = w_kpn(pw2)
        for half in range(2):
            sl, b_sl = get_w(f"pw2_{half}", [(8, 512, 0, 512, wp[:, :, half * 512:(half + 1) * 512], None)])
            for fcl in range(4):
                fc = half * 4 + fcl
                bank, b_bank = proj_fm(sl, b_sl, 512, fcl * 128, hT, b_hT)
                kb.stt(xt[:, fc, :], bank[:, :], gatecol(1, 0, fc), xt[:, fc, :], ALU.mult, ALU.add,
                       [b_bank, b_modc[1], b_xt[fc]], [b_xt[fc]])
                kb.ts(xt[:, fc, :], xt[:, fc, :], dcol[:, 32 + fc:33 + fc], None, ALU.add, None,
                      [b_xt[fc], b_dcol[1]], [b_xt[fc]])

    outs = []
    for t in range(nt):
        S.epoch = t + 1
        l0_mixer(t)
        dbg_dump(0, t)
        if dbg and t == 0:
            for c in range(8):
                kb.dma(dbgT[3, c, :, 0:T], hT[:, c, :], [b_hT[c]], [])
                kb.dma(dbgT[3, c, :, T:2 * T], yT[:, c, :], [b_yT[c]], [])
            for c in range(4):
                kb.dma(dbgT[3, c, :, 2 * T:3 * T], qT[:, c, :], [b_qT[c]], [])
                kb.dma(dbgT[3, c, :, 3 * T:4 * T], kT[:, c, 0:T], [b_kT[c][0]], [])
            kb.dma(dbgT[3, 4, :, 2 * T:3 * T], cst[:, 0, :], [b_cst], [])
            kb.dma(dbgT[3, 5, :, 2 * T:3 * T], cst[:, 1, :], [b_cst], [])
        ffn(0, t)
        dbg_dump(1, t)
        l1_mixer(t)
        dbg_dump(2, t)
        ffn(1, t)
        outs.append(kb.dma(outT[:, :, t * T:(t + 1) * T].rearrange("c p t -> p c t"), xt[:, :, :], b_xt, []))
        if t + 1 < nt:
            load_x(t + 1)
    S.emit(nc, final_deps=outs)
    _CACHE["sbuf_left"] = nc.sbuf_bytes_remaining
    kb.es.close()
    return nc


_CACHE = {}


def _prep_shared(inp):
    f = np.float32
    cols = np.zeros((128, NCOL), f)

    def put(name, arr):
        cols[:, COLS[name]:COLS[name] + arr.shape[1]] = arr

    def pc(v):
        return np.ascontiguousarray(v.reshape(-1, 128).T)

    put("adab", np.concatenate([pc(inp["ada_b"][l]) for l in range(2)], 1))
    put("gmix", np.concatenate([pc(inp["norm_mix_g"][l]) for l in range(2)], 1))
    put("gffn", np.concatenate([pc(inp["norm_ffn_g"][l]) for l in range(2)], 1))
    put("gq", np.tile(inp["b_q_norm_g"][0], 2)[:, None])
    put("gk", np.tile(inp["b_k_norm_g"][0], 2)[:, None])
    put("pw1b", pc(inp["conv_pw1_b"][0]))
    cdw = inp["conv_dw_w"][0]
    put("cdw", np.ascontiguousarray(cdw.reshape(31, 8, 128).transpose(2, 1, 0)).reshape(128, 248))
    put("cdb", pc(inp["conv_dw_b"][0]))
    put("lng", pc(inp["conv_ln_g"][0]))
    put("lnb", pc(inp["conv_ln_b"][0]))
    put("pw2b", pc(inp["conv_pw2_b"][0]))
    fdw = inp["ffn_dw_w"]
    put("fdw", np.ascontiguousarray(fdw.reshape(2, 3, NCH, 128).transpose(3, 0, 2, 1)).reshape(128, 264))
    put("fdb", np.ascontiguousarray(inp["ffn_dw_b"].reshape(2, NCH, 128).transpose(2, 0, 1)).reshape(128, 88))
    invf = (1.0 / (10000.0 ** (np.arange(0, 64, 2, dtype=np.float32) / 64.0))).astype(f)
    put("invf", np.tile(invf, 4)[:, None])
    put("eps", np.full((128, 1), EPS, f))
    put("halfpi", np.full((128, 1), 0.5 * math.pi, f))
    rows = np.zeros((128, 1024), f)
    rows[:, 0:512] = inp["a_vnorm_g"][0].reshape(1, 512)
    rows[:, 512:1024] = inp["a_spatial_b"][0].reshape(1, 512)
    wsT = np.ascontiguousarray(inp["a_spatial_w"][0].transpose(2, 0, 1))
    p = np.arange(128)
    wmask = (p[None, :] >= p[:, None]).astype(f)
    return {
        "ada_w": np.ascontiguousarray(inp["ada_w"]), "w_in": np.ascontiguousarray(inp["ab_w_in"][0]),
        "w_out": np.ascontiguousarray(inp["ab_w_out"][0]), "up_w": np.ascontiguousarray(inp["ffn_up_w"]),
        "down_w": np.ascontiguousarray(inp["ffn_down_w"]), "pw1": np.ascontiguousarray(inp["conv_pw1_w"][0]),
        "pw2": np.ascontiguousarray(inp["conv_pw2_w"][0]), "wsT": wsT, "cols": cols, "rows": rows,
        "cbf": _const_bf16(), "wmask": wmask,
    }


def _in_maps(inp, n_cores=8):
    inp = {k: np.asarray(v) for k, v in inp.items()}
    shared = _prep_shared(inp)
    maps = []
    for b in range(n_cores):
        m = dict(shared)
        m["xT"] = np.ascontiguousarray(inp["x"][b].T).reshape(8, 128, SEQ)
        m["cT"] = np.ascontiguousarray(inp["c"][b].reshape(8, 128).T)
        m["posr"] = np.ascontiguousarray(np.broadcast_to(inp["positions"][b].astype(np.int32)[None, :], (128, SEQ)))
        maps.append(m)
    return maps


def kernel(**inputs):
    if "nc" not in _CACHE:
        _CACHE["nc"] = build_program()
    nc = _CACHE["nc"]
    maps = _in_maps(inputs)
    res = run_bass_kernel_spmd(nc, maps, core_ids=list(range(8)))
    out = np.stack([r["outT"].reshape(D, SEQ).T for r in res.results], 0)
    return np.ascontiguousarray(out.astype(np.float32))
```

```python
import math
import numpy as np
import ml_dtypes
import concourse.bass as bass
import concourse.mybir as mybir
from concourse.bass_utils import run_bass_kernel_spmd
from contextlib import ExitStack

F32 = mybir.dt.float32
BF16 = mybir.dt.bfloat16
I32 = mybir.dt.int32
AF = mybir.ActivationFunctionType
ALU = mybir.AluOpType
AX = mybir.AxisListType

D = 1024
SEQ = 4096
T = 512
NT = SEQ // T
FF = 2816
NCH = 2 * FF // 128
EPS = 1e-6
NSLAB = 3
HV = 260


class Buf:
    __slots__ = ("name", "w", "r", "const")

    def __init__(self, name, const=False):
        self.name = name
        self.w = None
        self.r = []
        self.const = const


class Op:
    __slots__ = ("eng", "fn", "deps", "idx", "sig", "signo", "dma", "sem_key", "epoch")


class Sched:
    ENGS = ("pe", "act", "dve", "pool", "sp")
    NDMA = {"sp": 16, "pool": 6, "act": 4}

    def __init__(self):
        self.ops = []
        self.epoch = 0
        self.dma_count = {"sp": 0, "pool": 0, "act": 0}
        self.dma_last = {}

    def add(self, eng, fn, reads=(), writes=(), dma=False):
        i = len(self.ops)
        op = Op()
        op.eng, op.fn, op.idx, op.dma, op.sig, op.signo = eng, fn, i, dma, False, 0
        op.epoch = self.epoch
        deps = set()
        for b in reads:
            if b.w is not None:
                deps.add(b.w)
        for b in writes:
            if b.w is not None:
                deps.add(b.w)
            deps.update(b.r)
        for b in reads:
            if not b.const:
                b.r.append(i)
        for b in writes:
            b.w = i
            b.r = []
        deps.discard(i)
        if dma:
            k = self.dma_count[eng]
            self.dma_count[eng] = k + 1
            op.sem_key = ("dma", eng, k % self.NDMA[eng])
            prev = self.dma_last.get(op.sem_key)
            if prev is not None:
                deps.add(prev)
            self.dma_last[op.sem_key] = i
        else:
            op.sem_key = ("eng", eng, self.epoch)
        if eng == "pe" and not dma:
            deps = {d for d in deps if not (self.ops[d].eng == "pe" and not self.ops[d].dma)}
        best = {}
        out = set()
        for d in deps:
            o = self.ops[d]
            if o.dma:
                out.add(d)
            else:
                if o.sem_key not in best or best[o.sem_key] < d:
                    best[o.sem_key] = d
        out.update(best.values())
        op.deps = out
        self.ops.append(op)
        return i

    def emit(self, nc, final_deps=()):
        ops = self.ops
        for op in ops:
            for d in op.deps:
                ops[d].sig = True
        for d in final_deps:
            ops[d].sig = True
        counters = {}
        for op in ops:
            if op.sig:
                c = counters.get(op.sem_key, 0) + (16 if op.dma else 1)
                counters[op.sem_key] = c
                op.signo = c
        with ExitStack() as es:
            sems = {k: es.enter_context(nc.semaphore("s_" + "_".join(map(str, k)))) for k in counters}
            block = es.enter_context(nc.Block())
            per_eng = {e: [op for op in ops if op.eng == e] for e in self.ENGS}

            def body(eng_name, eng):
                seen = {}

                def wait_all(deps):
                    need = {}
                    for d in deps:
                        o = ops[d]
                        if need.get(o.sem_key, 0) < o.signo:
                            need[o.sem_key] = o.signo
                    for k, v in need.items():
                        if seen.get(k, 0) >= v:
                            continue
                        eng.wait_ge(sems[k], v)
                        seen[k] = v

                for op in per_eng[eng_name]:
                    wait_all(op.deps)
                    inst = op.fn(eng)
                    if op.sig:
                        inst.then_inc(sems[op.sem_key], 16 if op.dma else 1)
                if eng_name == "sp":
                    wait_all(final_deps)

            @block.tensor
            def _(e):
                body("pe", e)

            @block.scalar
            def _(e):
                body("act", e)

            @block.vector
            def _(e):
                body("dve", e)

            @block.gpsimd
            def _(e):
                body("pool", e)

            @block.sync
            def _(e):
                body("sp", e)


def _cols_layout():
    names = [("adab", 96), ("gmix", 16), ("gffn", 16), ("gq", 1), ("gk", 1), ("pw1b", 16), ("cdw", 248),
             ("cdb", 8), ("lng", 8), ("lnb", 8), ("pw2b", 8), ("fdw", 264), ("fdb", 88), ("invf", 1),
             ("eps", 1), ("halfpi", 1)]
    off = {}
    o = 0
    for n, w in names:
        off[n] = o
        o += w
    return off, o


COLS, NCOL = _cols_layout()
CB = {"bd64": 0, "rm": 128, "ones1024": 256, "m1a": 384, "m1b": 896, "m4p": 1408, "m4c": 1920,
      "m16_0": 2432, "m16_1": 2944, "m16_2": 3456, "m16_3": 3968, "m16o": 4480, "ident": 4992, "shift": 5120}
NCB = 5248


def _const_bf16():
    c = np.zeros((128, NCB), np.float32)
    p = np.arange(128)
    c[:, 0:128] = ((p[:, None] // 64) == (p[None, :] // 64)) * (1.0 / 64.0)
    rm = np.zeros((128, 128), np.float32)
    for m in range(128):
        if m % 64 < 32:
            rm[m + 32, m] = -1.0
        else:
            rm[m - 32, m] = 1.0
    c[:, 128:256] = rm
    c[:, 256:384] = 1.0 / 1024.0
    k = p[:, None]
    q = np.arange(256)[None, :]
    m1 = ((q - k >= 0) & (q - k <= 128)).astype(np.float32)
    c[:, 384:896] = np.concatenate([m1[:, 128:256], m1[:, 0:256], m1[:, 0:128]], 1)
    c[:, 896:1408] = np.concatenate([m1, m1], 1)
    c[:, 1408:1920] = np.tile(m1[:, 128:256], (1, 4))
    c[:, 1920:2432] = np.tile(m1[:, 0:128], (1, 4))
    j = np.arange(32)[None, :]
    for tt in range(4):
        ma = (k <= 32 * tt + j).astype(np.float32)
        c[:, 2432 + 512 * tt: 2432 + 512 * (tt + 1)] = np.tile(ma, (1, 16))
    mo = ((k >= j) & (k < 32)).astype(np.float32)
    c[:, 4480:4992] = np.tile(mo, (1, 16))
    c[:, 4992:5120] = np.eye(128, dtype=np.float32)
    sh = np.zeros((128, 128), np.float32)
    for kk in range(64):
        sh[kk, kk + 64] = 1.0
    c[:, 5120:5248] = sh
    return c.astype(ml_dtypes.bfloat16)


class KB:
    def __init__(self, nc):
        self.nc = nc
        self.S = Sched()
        self.es = ExitStack()
        self.rot = {}

    def sb(self, name, shape, dt):
        return self.es.enter_context(self.nc.sbuf_tensor("s_" + name, shape, dt))

    def ps(self, name, shape, dt):
        return self.es.enter_context(self.nc.psum_tensor("p_" + name, shape, dt))

    def add(self, eng, fn, r=(), w=(), dma=False):
        return self.S.add(eng, fn, r, w, dma)

    def mm(self, out, lhsT, rhs, start, stop, r, w, skip=False):
        if skip:
            return self.add("pe", lambda e: e.matmul(out, lhsT=lhsT, rhs=rhs, start=start, stop=stop,
                                                     skip_group_check=True), r, w)
        return self.add("pe", lambda e: e.matmul(out, lhsT=lhsT, rhs=rhs, start=start, stop=stop), r, w)

    def act(self, out, in_, func, r, w, scale=1.0, bias=None):
        if bias is None:
            return self.add("act", lambda e: e.activation(out=out, in_=in_, func=func, scale=scale), r, w)
        return self.add("act", lambda e: e.activation(out=out, in_=in_, func=func, scale=scale, bias=bias), r, w)

    def tt(self, out, in0, in1, op, r, w, eng="dve"):
        return self.add(eng, lambda e: e.tensor_tensor(out=out, in0=in0, in1=in1, op=op), r, w)

    def ts(self, out, in0, s1, s2, op0, op1, r, w, eng="dve"):
        if s2 is None:
            return self.add(eng, lambda e: e.tensor_scalar(out=out, in0=in0, scalar1=s1, scalar2=None, op0=op0), r, w)
        return self.add(eng, lambda e: e.tensor_scalar(out=out, in0=in0, scalar1=s1, scalar2=s2, op0=op0, op1=op1), r, w)

    def stt(self, out, in0, scalar, in1, op0, op1, r, w, eng="dve"):
        return self.add(eng, lambda e: e.scalar_tensor_tensor(out=out, in0=in0, scalar=scalar, in1=in1, op0=op0, op1=op1), r, w)

    def copy(self, out, in_, r, w, eng="dve"):
        return self.add(eng, lambda e: e.tensor_copy(out=out, in_=in_), r, w)

    def recip(self, out, in_, r, w):
        return self.add("dve", lambda e: e.reciprocal(out=out, in_=in_), r, w)

    def memset(self, ap, val, w, eng="dve"):
        return self.add(eng, lambda e: e.memset(ap, val), (), w)

    def dma(self, out, in_, r, w, eng="sp"):
        return self.add(eng, lambda e: e.dma_start(out=out, in_=in_), r, w, dma=True)

    def rr(self, key, n):
        i = self.rot.get(key, 0)
        self.rot[key] = i + 1
        return i % n


def build_program(nt=NT, dbg=False):
    nc = bass.Bass("TRN2", target_bir_lowering=False)
    kb = KB(nc)
    S = kb.S

    def din(name, shape, dt):
        return nc.dram_tensor(name, shape, dt, kind="ExternalInput").ap()

    def dscr(name, shape, dt):
        return nc.dram_tensor(name, shape, dt, kind="Internal").ap()

    xT = din("xT", [8, 128, SEQ], F32)
    cT = din("cT", [128, 8], F32)
    posr = din("posr", [128, SEQ], I32)
    ada_w = din("ada_w", [2, D, 6 * D], F32)
    w_in = din("w_in", [D, 2560], F32)
    w_out = din("w_out", [D, D], F32)
    up_w = din("up_w", [2, D, 2 * FF], F32)
    down_w = din("down_w", [2, FF, D], F32)
    pw1 = din("pw1", [D, 2 * D], F32)
    pw2 = din("pw2", [D, D], F32)
    wsT = din("wsT", [128, 4, 128], F32)
    cols_d = din("cols", [128, NCOL], F32)
    rows_d = din("rows", [128, 1024], F32)
    cbf_d = din("cbf", [128, NCB], BF16)
    wmask_d = din("wmask", [128, 128], F32)
    outT = nc.dram_tensor("outT", [8, 128, SEQ], F32, kind="ExternalOutput").ap()
    if dbg:
        dbgT = nc.dram_tensor("dbgT", [4, 8, 128, SEQ], F32, kind="ExternalOutput").ap()

    v_scr = dscr("v_scr", [SEQ, 2, HV], BF16)

    xt = kb.sb("xt", [128, 8, T], F32)
    b_xt = [Buf(f"xt{c}") for c in range(8)]
    hT = kb.sb("hT", [128, 8, T], BF16)
    b_hT = [Buf(f"hT{c}") for c in range(8)]
    slabs = [kb.sb(f"slab{i}", [128, 4096], BF16) for i in range(NSLAB)]
    b_slab = [[Buf(f"slab{i}a"), Buf(f"slab{i}b")] for i in range(NSLAB)]
    kT = kb.sb("kT", [128, 4, SEQ], BF16)
    b_kT = [[Buf(f"kT{c}_{t}") for t in range(NT)] for c in range(4)]
    qT = kb.sb("qT", [128, 4, T], BF16)
    b_qT = [Buf(f"qT{c}") for c in range(4)]
    yT = kb.sb("yT", [128, 8, T], BF16)
    b_yT = [Buf(f"yT{c}") for c in range(8)]
    ua = kb.sb("ua", [128, 4, T], F32)
    b_ua = [Buf(f"ua{c}") for c in range(4)]
    vn = kb.sb("vn", [128, 4, T], BF16)
    b_vn = [Buf(f"vn{c}") for c in range(4)]
    arena = kb.sb("arena", [128, 6656], F32)
    ar16 = arena[:, :].bitcast(BF16)
    V1 = ar16[:, 0:1300].rearrange("p (b c) -> p b c", b=5)
    V4 = ar16[:, 1300:3380].rearrange("p (b c) -> p b c", b=8)
    V16A = ar16[:, 3380:7540].rearrange("p (r c) -> p r c", r=16)
    V16O = ar16[:, 7540:11700].rearrange("p (r c) -> p r c", r=16)
    b_V1 = [Buf("V1")]
    b_V4 = [Buf("V4p"), Buf("V4c")]
    b_V16A = Buf("V16A")
    b_V16O = Buf("V16O")
    gT = ar16[:, 0:22 * T].rearrange("p (k t) -> p k t", k=22)
    b_gT = [Buf(f"gT{k}") for k in range(22)]
    yglu = ar16[:, 0:8 * 542].rearrange("p (c t) -> p c t", c=8)
    b_yglu = [Buf(f"yglu{c}") for c in range(8)]
    ycv = arena[:, 2560:2560 + 4096].rearrange("p (c t) -> p c t", c=8)
    b_ycv = [Buf(f"ycv{c}") for c in range(8)]
    arena_bufs = b_V1 + b_V4 + [b_V16A, b_V16O] + b_gT + b_yglu + b_ycv

    cbf = kb.sb("cbf", [128, NCB], BF16)
    b_cbf = Buf("cbf", const=True)
    cols = kb.sb("cols", [128, NCOL], F32)
    b_cols = Buf("cols", const=True)
    rows = kb.sb("rows", [128, 1024], F32)
    b_rows = Buf("rows", const=True)
    wsb = kb.sb("wsb", [128, 4, 128], BF16)
    b_wsb = Buf("wsb", const=True)
    onesf = kb.sb("onesf", [128, 128], F32)
    b_onesf = Buf("onesf", const=True)
    modc = kb.sb("modc", [128, 96], F32)
    b_modc = [Buf("modc0", const=True), Buf("modc1", const=True)]
    dcol = kb.sb("dcol", [128, 48], F32)
    b_dcol = [Buf("dcol0", const=True), Buf("dcol1", const=True)]
    cact = kb.sb("cact", [128, 8], BF16)
    b_cact = Buf("cact", const=True)
    NTF = 6
    tf = [kb.sb(f"tf{i}", [128, T], F32) for i in range(NTF)]
    b_tf = [Buf(f"tf{i}") for i in range(NTF)]
    NTB = 6
    tb = [kb.sb(f"tb{i}", [128, T], BF16) for i in range(NTB)]
    b_tb = [Buf(f"tb{i}") for i in range(NTB)]
    zs = [kb.sb(f"zs{i}", [128, T + 2], F32) for i in range(3)]
    b_zs = [Buf(f"zs{i}") for i in range(3)]
    b_zh = [Buf(f"zh{i}") for i in range(3)]
    acc = [kb.sb(f"acc{i}", [128, T], F32) for i in range(3)]
    b_acc = [Buf(f"acc{i}") for i in range(3)]
    sa = [kb.sb(f"sa{i}", [128, T], F32) for i in range(2)]
    b_sa = [Buf(f"sa{i}") for i in range(2)]
    halo = kb.sb("halo", [128, 2 * NCH * 2], F32)
    b_halo = [[Buf(f"halo{l}_{c}") for c in range(NCH)] for l in range(2)]
    vst = kb.sb("vst", [128, 4, 2 * HV], BF16)
    b_vst = [Buf(f"vst{i}") for i in range(4)]
    cst = kb.sb("cst", [128, 2, T], F32)
    b_cst = Buf("cst")
    posf = kb.sb("posf", [128, T], I32)
    b_posf = Buf("posf")
    yhalo = kb.sb("yhalo", [128, 8, 30], BF16)
    b_yhalo = [Buf(f"yhalo{c}") for c in range(8)]
    small = kb.sb("small", [128, 96], F32)
    b_small = Buf("small")
    b_s1 = [Buf(f"s1_{i}") for i in range(4)]
    b_s2 = [Buf(f"s2_{i}") for i in range(4)]
    b_sm, b_sv, b_snb = Buf("sm"), Buf("sv"), Buf("snb")
    rstd_t = kb.sb("rstd_t", [128, T], F32)
    b_rstd_t = Buf("rstd_t")
    ln_mu = kb.sb("ln_mu", [128, T], F32)
    b_ln_mu = Buf("ln_mu")
    rd, b_rd = ln_mu, b_ln_mu

    pbank = [kb.ps(f"pb{i}", [128, T], F32) for i in range(8)]
    b_pb = [Buf(f"pb{i}") for i in range(8)]
    PCL = {"A": [0, 1, 2], "B": [3, 4, 5], "C": [6], "D": [7], "O": [6, 7]}

    def psum(cl):
        lst = PCL[cl]
        i = lst[kb.rr("ps" + cl, len(lst))]
        return pbank[i], b_pb[i]

    def tmpf():
        i = kb.rr("tf", NTF)
        return tf[i], b_tf[i]

    def tmpb():
        i = kb.rr("tb", NTB)
        return tb[i], b_tb[i]

    def col(name, i=0, n=1):
        o = COLS[name] + i
        return cols[:, o:o + n]

    def cb(name, w=512):
        o = CB[name]
        return cbf[:, o:o + w]

    def load_x(t):
        for c in range(8):
            kb.dma(xt[:, c, :], xT[c, :, t * T:(t + 1) * T], [], [b_xt[c]])

    load_x(0)
    kb.dma(cbf[:, :], cbf_d, [], [b_cbf])
    kb.dma(cols[:, :], cols_d, [], [b_cols])
    kb.dma(rows[:, :], rows_d, [], [b_rows])
    ti, b_ti = tmpf()
    kb.dma(ti[:, 0:128], wmask_d, [], [b_ti])
    t2, b_t2 = tmpf()
    kb.dma(t2[:, :], wsT.rearrange("p g t -> p (g t)"), [], [b_t2])
    for g in range(4):
        kb.tt(wsb[:, g, :], t2[:, g * 128:(g + 1) * 128], ti[:, 0:128], ALU.mult, [b_t2, b_ti], [b_wsb])
    kb.memset(onesf[:, :], 1.0, [b_onesf])
    kb.memset(halo[:, :], 0.0, [b for l in b_halo for b in l])
    kb.memset(vst[:, :, :], 1.0, b_vst)
    b_w = {}

    t3, b_t3 = tmpf()
    kb.dma(t3[:, 0:8], cT, [], [b_t3])
    kb.act(cact[:, :], t3[:, 0:8], AF.Silu, [b_t3], [b_cact])
    slab_state = {"n": 0}

    def next_slab():
        i = slab_state["n"] % NSLAB
        slab_state["n"] += 1
        return slabs[i], b_slab[i]

    modp, b_modp = psum("C")

    def ada_slab(l, n):
        sl, b_sl = next_slab()
        kb.dma(sl[:, :].rearrange("p (k n) -> p k n", k=8),
               ada_w[l].rearrange("(k p) n -> p k n", p=128)[:, :, n * 512:(n + 1) * 512], [], b_sl, eng="pool")
        for jj in range(4):
            j = 4 * n + jj
            for kc in range(8):
                kb.mm(modp[:, l * 48 + j: l * 48 + j + 1],
                      sl[:, kc * 512 + jj * 128: kc * 512 + (jj + 1) * 128], cact[:, kc:kc + 1],
                      kc == 0, kc == 7, b_sl + [b_cact], [b_modp])

    def ada_finish(l):
        kb.tt(modc[:, l * 48:(l + 1) * 48], modp[:, l * 48:(l + 1) * 48], col("adab", l * 48, 48), ALU.add,
              [b_modp, b_cols], [b_modc[l]])
        for sub in range(2):
            o = (l * 2 + sub) * 8
            gname = "gmix" if sub == 0 else "gffn"
            kb.stt(dcol[:, o:o + 8], modc[:, l * 48 + (3 * sub + 1) * 8: l * 48 + (3 * sub + 1) * 8 + 8], 1.0,
                   col(gname, l * 8, 8), ALU.add, ALU.mult, [b_modc[l], b_cols], [b_dcol[l]])
        if l == 1:
            kb.tt(dcol[:, 32:40], col("pw2b", 0, 8), modc[:, 48 + 16: 48 + 24], ALU.mult, [b_modc[1], b_cols],
                  [b_dcol[1]])

    for l_ in range(2):
        for n in range(12):
            ada_slab(l_, n)
        ada_finish(l_)

    def gmcol(l, sub, c):
        o = (l * 2 + sub) * 8 + c
        return dcol[:, o:o + 1]

    def shcol(l, sub, c):
        o = l * 48 + (3 * sub) * 8 + c
        return modc[:, o:o + 1]

    def gatecol(l, sub, c):
        o = l * 48 + (3 * sub + 2) * 8 + c
        return modc[:, o:o + 1]

    wscr = dscr("wscr", [64, 128, 4096], BF16)
    slab_ids = {}
    b_wscr = {}

    def get_w(key, parts, build=None):
        sl, b_sl = next_slab()
        used = max([k * ncols for (k, ncols, c0, w, src, bi) in parts] + ([31 * 128] if build is not None else []))
        if key not in slab_ids:
            sid = len(slab_ids)
            slab_ids[key] = sid
            if build is not None:
                build(sl, b_sl)
            for (k, ncols, c0, w, src, bi) in parts:
                view = sl[:, 0:k * ncols].rearrange("p (k n) -> p k n", k=k)[:, :, c0:c0 + w]
                kb.dma(view, src, [], b_sl if bi is None else [b_sl[bi]], eng="pool")
            b = Buf(f"wscr{sid}")
            b_wscr[sid] = b
            kb.dma(wscr[sid][:, 0:used], sl[:, 0:used], b_sl, [b], eng="sp")
        else:
            sid = slab_ids[key]
            kb.dma(sl[:, 0:used], wscr[sid][:, 0:used], [b_wscr[sid]], b_sl, eng="sp")
        return sl, b_sl

    def w_kpn(w2d):
        return w2d.rearrange("(k p) n -> p k n", p=128)

    def norm_mod(l, sub):
        ssb, b_ssb = psum("C")
        for c in range(8):
            sq, b_sq = tmpb()
            kb.act(sq[:, :], xt[:, c, :], AF.Square, [b_xt[c]], [b_sq])
            kb.mm(ssb[:, :], cb("ones1024", 128), sq[:, :], c == 0, c == 7, [b_sq, b_cbf], [b_ssb])
        sd, b_sd = tmpf()
        kb.act(sd[:, :], ssb[:, :], AF.Ln, [b_ssb, b_cols], [b_sd], bias=col("eps"))
        rs, b_rs = rstd_t, b_rstd_t
        kb.act(rs[:, :], sd[:, :], AF.Exp, [b_sd], [b_rs], scale=-0.5)
        for c in range(8):
            t, b_t = tmpf()
            kb.stt(t[:, :], xt[:, c, :], gmcol(l, sub, c), rs[:, :], ALU.mult, ALU.mult,
                   [b_xt[c], b_rs, b_dcol[l]], [b_t])
            kb.act(hT[:, c, :], t[:, :], AF.Identity, [b_t, b_modc[l]], [b_hT[c]], bias=shcol(l, sub, c))

    def proj_fm(sl, b_sl, ncols, col0, src, b_src, nk=8):
        bank, b_bank = psum("A")
        for kc in range(nk):
            kb.mm(bank[:, :], sl[:, kc * ncols + col0: kc * ncols + col0 + 128], src[:, kc, :],
                  kc == 0, kc == nk - 1, b_sl + [b_src[kc]], [b_bank])
        return bank, b_bank

    def proj_tm(sl, b_sl, tc, src, b_src):
        bank, b_bank = psum("A")
        for kc in range(8):
            kb.mm(bank[:, :], src[:, kc, tc * 128:(tc + 1) * 128], sl[:, kc * 512:(kc + 1) * 512],
                  kc == 0, kc == 7, b_sl + [b_src[kc]], [b_bank])
        return bank, b_bank

    def arena_barrier():
        kb.memset(small[:, 90:91], 0.0, [b_small] + arena_bufs)

    def dbg_dump(stage, t):
        if dbg:
            for c in range(8):
                kb.dma(dbgT[stage, c, :, t * T:(t + 1) * T], xt[:, c, :], [b_xt[c]], [])

    def qk_stages(sl, b_sl, fc, gname, dst, b_dst):
        st = {}

        def sA():
            st["qp"] = proj_fm(sl, b_sl, 512, fc * 128, hT, b_hT)
            qp, b_qp = st["qp"]
            st["sq"] = tmpb()
            sq, b_sq = st["sq"]
            kb.act(sq[:, :], qp[:, :], AF.Square, [b_qp], [b_sq])

        def sB():
            qp, b_qp = st["qp"]
            sq, b_sq = st["sq"]
            ssb, b_ssb = psum("C")
            kb.mm(ssb[:, :], cb("bd64", 128), sq[:, :], True, True, [b_sq, b_cbf], [b_ssb])
            sd, b_sd = tmpf()
            kb.act(sd[:, :], ssb[:, :], AF.Ln, [b_ssb, b_cols], [b_sd], bias=col("eps"))
            rs, b_rs = tmpf()
            kb.act(rs[:, :], sd[:, :], AF.Exp, [b_sd], [b_rs], scale=-0.5)
            st["q1"] = tmpb()
            q1, b_q1 = st["q1"]
            kb.stt(q1[:, :], qp[:, :], col(gname), rs[:, :], ALU.mult, ALU.mult, [b_qp, b_rs, b_cols], [b_q1])

        def sC():
            q1, b_q1 = st["q1"]
            rot, b_rot = psum("D")
            kb.mm(rot[:, :], cb("rm", 128), q1[:, :], True, True, [b_q1, b_cbf], [b_rot])
            t1, b_t1 = tmpf()
            kb.tt(t1[:, :], q1[:, :], cst[:, 0, :], ALU.mult, [b_q1, b_cst], [b_t1])
            t2_, b_t2_ = tmpf()
            kb.tt(t2_[:, :], rot[:, :], cst[:, 1, :], ALU.mult, [b_rot, b_cst], [b_t2_])
            kb.tt(dst, t1[:, :], t2_[:, :], ALU.add, [b_t1, b_t2_], [b_dst])

        return [sA, sB, sC]

    def l0_mixer(t):
        arena_barrier()
        norm_mod(0, 0)
        kb.dma(posf[:, :], posr[:, t * T:(t + 1) * T], [], [b_posf])
        ang, b_ang = tmpf()
        kb.copy(ang[:, :], posf[:, :], [b_posf], [b_ang])
        kb.ts(ang[:, :], ang[:, :], col("invf"), None, ALU.mult, None, [b_ang, b_cols], [b_ang])
        u_, b_u = tmpf()
        kb.ts(u_[:, :], ang[:, :], 1.0 / (2.0 * math.pi), None, ALU.mult, None, [b_ang], [b_u])
        kb.copy(posf[:, :], u_[:, :], [b_u], [b_posf])
        kb.copy(u_[:, :], posf[:, :], [b_posf], [b_u])
        r_, b_r = tmpf()
        kb.stt(r_[:, :], u_[:, :], -2.0 * math.pi, ang[:, :], ALU.mult, ALU.add, [b_u, b_ang], [b_r])
        sh_, b_sh = tmpf()
        kb.act(sh_[:, :], r_[:, :], AF.Sin, [b_r], [b_sh], scale=0.5)
        ch_, b_ch = tmpf()
        kb.act(ch_[:, :], r_[:, :], AF.Sin, [b_r, b_cols], [b_ch], scale=-0.5, bias=col("halfpi"))
        kb.stt(cst[:, 1, :], sh_[:, :], 2.0, ch_[:, :], ALU.mult, ALU.mult, [b_sh, b_ch], [b_cst])
        kb.tt(sh_[:, :], sh_[:, :], sh_[:, :], ALU.mult, [b_sh], [b_sh])
        kb.ts(cst[:, 0, :], sh_[:, :], -2.0, 1.0, ALU.mult, ALU.add, [b_sh], [b_cst])

        wk = w_kpn(w_in)
        sl, b_sl = get_w("win0", [(8, 512, 0, 512, wk[:, :, 0:512], None)])
        for fc in range(4):
            bank, b_bank = proj_fm(sl, b_sl, 512, fc * 128, hT, b_hT)
            kb.act(ua[:, fc, :], bank[:, :], AF.Gelu, [b_bank], [b_ua[fc]])
        sl, b_sl = get_w("win512", [(8, 512, 0, 512, wk[:, :, 512:1024], None)])
        S1, S2, M_, V_ = small[:, 0:16], small[:, 16:32], small[:, 32:48], small[:, 48:64]
        kb.memset(S2, 0.0, b_s2)
        vgs = []
        for tc in range(4):
            bank, b_bank = proj_tm(sl, b_sl, tc, hT, b_hT)
            vg, b_vg = tmpf()
            kb.act(vg[:, :], bank[:, :], AF.Gelu, [b_bank], [b_vg])
            kb.add("dve", lambda e, tc=tc, vg=vg: e.tensor_reduce(
                out=small[:, tc * 4:(tc + 1) * 4], in_=vg[:, :].rearrange("p (g d) -> p g d", g=4), axis=AX.X,
                op=ALU.add), [b_vg], [b_s1[tc]])
            for g in range(4):
                junk, b_junk = tmpb()
                kb.add("act", lambda e, tc=tc, g=g, vg=vg, junk=junk: e.activation(
                    out=junk[:, 0:128], in_=vg[:, g * 128:(g + 1) * 128], func=AF.Square,
                    accum_out=small[:, 16 + tc * 4 + g: 17 + tc * 4 + g]), [b_vg], [b_junk, b_s2[tc]])
            vgs.append((vg, b_vg))
        kb.ts(M_, S1, 1.0 / 128.0, None, ALU.mult, None, b_s1, [b_sm])
        kb.tt(V_, M_, M_, ALU.mult, [b_sm], [b_sv])
        kb.stt(V_, S2, 1.0 / 128.0, V_, ALU.mult, ALU.subtract, b_s2 + [b_sv], [b_sv])
        kb.act(V_, V_, AF.Ln, [b_sv, b_cols], [b_sv], bias=col("eps"))
        kb.act(V_, V_, AF.Exp, [b_sv], [b_sv], scale=-0.5)
        kb.stt(S1, M_, -1.0, V_, ALU.mult, ALU.mult, [b_sm, b_sv] + b_s1, [b_snb])
        for tc in range(4):
            vg, b_vg = vgs[tc]
            for g in range(4):
                o = tc * 4 + g
                kb.act(vg[:, g * 128:(g + 1) * 128], vg[:, g * 128:(g + 1) * 128], AF.Identity,
                       [b_vg, b_sv, b_snb], [b_vg], scale=small[:, 48 + o:49 + o], bias=small[:, o:o + 1])
            kb.tt(vn[:, tc, :], vg[:, :], rows[:, 0:512], ALU.mult, [b_vg, b_rows], [b_vn[tc]])
        slq, b_slq = get_w("win1024", [(8, 512, 0, 512, wk[:, :, 1024:1536], None)])
        slk, b_slk = get_w("win1536", [(8, 512, 0, 512, wk[:, :, 1536:2048], None)])
        stg = [qk_stages(slq, b_slq, fc, "gq", qT[:, fc, :], b_qT[fc]) for fc in range(4)]
        stg += [qk_stages(slk, b_slk, fc, "gk", kT[:, fc, t * T:(t + 1) * T], b_kT[fc][t]) for fc in range(4)]
        n_ = len(stg)
        for i in range(n_ + 2):
            if i < n_:
                stg[i][0]()
            if 0 <= i - 1 < n_:
                stg[i - 1][1]()
            if 0 <= i - 2 < n_:
                stg[i - 2][2]()
        sl, b_sl = get_w("win2048", [(8, 512, 0, 512, wk[:, :, 2048:2560], None)])
        b_vrow = []
        for tc in range(4):
            bank, b_bank = proj_tm(sl, b_sl, tc, hT, b_hT)
            kb.act(vst[:, tc, :].rearrange("p (h e) -> p h e", e=65)[:, :, 0:64],
                   bank[:, :].rearrange("p (h e) -> p h e", e=64), AF.Identity, [b_bank], [b_vst[tc]])
            b = Buf(f"vrow{t}_{tc}")
            kb.dma(v_scr[t * T + tc * 128: t * T + (tc + 1) * 128].rearrange("p a c -> p (a c)"), vst[:, tc, :],
                   [b_vst[tc]], [b])
            b_vrow.append(b)
        b_vtile[t] = b_vrow

        ks = max(0, 32 * (t - 3))
        nk = 32 * (t + 1) - ks

        def head_tasks(hl, h):
            c = h // 2
            pb = 64 * (h % 2)
            vc = slice(hl * 65, hl * 65 + 65)
            O, b_O = psum("O")
            first = [True]
            kTc = kT[pb:pb + 64, c, :]
            qTc = qT[pb:pb + 64, c, :]
            rdq = [b_qT[c]]
            tasks = []

            def pv(ocols, lhsT, rhs, rds):
                if isinstance(ocols, tuple):
                    oap = O[0:65, :].rearrange("p (i s) -> p i s", s=ocols[1])[:, :, ocols[0]]
                else:
                    oap = O[0:65, ocols]
                kb.mm(oap, lhsT, rhs, first[0], False, rds, [b_O], skip=True)
                first[0] = False

            def add_task(qk_list, np_, c0, mask, pv_list):
                st = {}

                def qk():
                    Sb, b_Sb = psum("B")
                    st["S"] = (Sb, b_Sb)
                    for (oc, lh, rh, rds) in qk_list:
                        kb.mm(Sb[0:np_, oc], lh, rh, True, True, rds, [b_Sb])

                def post():
                    Sb, b_Sb = st["S"]
                    P, b_P = tmpb()
                    kb.act(P[0:np_, c0:512], Sb[0:np_, c0:512], AF.Exp, [b_Sb], [b_P], scale=0.125)
                    kb.tt(P[0:np_, c0:512], P[0:np_, c0:512], mask[0:np_, c0:512], ALU.mult, [b_P, b_cbf], [b_P])
                    for (ocols, lh, pc, rds) in pv_list:
                        pv(ocols, lh, P[0:np_, pc], [b_P] + rds)

                tasks.append([qk, post])

            units1 = [(4 * t - 1, 0, 0, 128), (4 * t, 128, 0, 256), (4 * t + 3, 384, 384, 128)]
            units2 = [(4 * t + 1, 0, 128, 256), (4 * t + 2, 256, 256, 256)]
            for ui, units in enumerate((units1, units2)):
                units = [u for u in units if u[0] >= 0]
                lo = min(u[1] for u in units)
                qk_list = [(slice(bc, bc + w), kTc[:, kbk * 128:(kbk + 1) * 128], qTc[:, q0:q0 + w],
                            [b_kT[c][kbk // 4]] + rdq) for (kbk, bc, q0, w) in units]
                pv_list = [(slice(q0, q0 + w), V1[:, kbk - (4 * t - 1), vc], slice(bc, bc + w), [b_V1[0]])
                           for (kbk, bc, q0, w) in units]
                add_task(qk_list, 128, lo, cb("m1a" if ui == 0 else "m1b"), pv_list)
            for which in (0, 1):
                tk = t - 1 + which
                if tk < 0:
                    continue
                qk_list = [(slice(r * 128, (r + 1) * 128),
                            kTc[:, tk * T:(tk + 1) * T].rearrange("p (i s) -> p i s", s=4)[:, :, r],
                            qTc.rearrange("p (i s) -> p i s", s=4)[:, :, r], [b_kT[c][tk]] + rdq) for r in range(4)]
                pv_list = [((r, 4), V4[:, 4 * which + r, vc], slice(r * 128, (r + 1) * 128), [b_V4[which]])
                           for r in range(4)]
                add_task(qk_list, 128, 0, cb("m4p" if which == 0 else "m4c"), pv_list)
            rdk = [b_kT[c][tt_] for tt_ in range(max(0, t - 3), t + 1)]
            qk_list = [(slice(r * 32, (r + 1) * 32),
                        kTc[:, 16 * ks:16 * (ks + nk)].rearrange("p (i s) -> p i s", s=16)[:, :, r],
                        qTc.rearrange("p (i s) -> p i s", s=16)[:, :, r], rdk + rdq) for r in range(16)]
            pv_list = [((r, 16), V16A[0:nk, r, vc], slice(r * 32, (r + 1) * 32), [b_V16A]) for r in range(16)]
            add_task(qk_list, nk, 0, cb(f"m16_{min(t, 3)}"), pv_list)
            if t >= 4:
                qk_list = [(slice(r * 32, (r + 1) * 32),
                            kTc[:, T * (t - 4):T * (t - 3)].rearrange("p (i s) -> p i s", s=16)[:, :, r],
                            qTc.rearrange("p (i s) -> p i s", s=16)[:, :, r], [b_kT[c][t - 4]] + rdq)
                           for r in range(16)]
                pv_list = [((r, 16), V16O[0:32, r, vc], slice(r * 32, (r + 1) * 32), [b_V16O]) for r in range(16)]
                add_task(qk_list, 32, 0, cb("m16o"), pv_list)

            def finish():
                kb.act(rd[64:65, :], O[64:65, :], AF.Ln, [b_O], [b_rd])
                kb.act(rd[64:65, :], rd[64:65, :], AF.Exp, [b_rd], [b_rd], scale=-1.0)
                BC, b_BC = psum("A")
                kb.mm(BC[0:64, :], onesf[64:65, 0:64], rd[64:65, :], True, True, [b_rd, b_onesf], [b_BC])
                on, b_on = tmpf()
                kb.act(on[0:64, :], O[0:64, :], AF.Identity, [b_O], [b_on])
                if pb == 0:
                    kb.tt(yT[0:64, 4 + c, :], on[0:64, :], BC[0:64, :], ALU.mult, [b_on, b_BC], [b_yT[4 + c]])
                else:
                    yb, b_yb = tmpb()
                    kb.tt(yb[0:64, :], on[0:64, :], BC[0:64, :], ALU.mult, [b_on, b_BC], [b_yb])
                    SH, b_SH = psum("A")
                    kb.mm(SH[:, :], cbf[0:64, CB["shift"]:CB["shift"] + 128], yb[0:64, :], True, True,
                          [b_yb, b_cbf], [b_SH])
                    kb.copy(yT[64:128, 4 + c, :], SH[64:128, :], [b_SH], [b_yT[4 + c]])

            tasks[-1].append(finish)
            return tasks

        for hh in range(2):
            rd2 = b_vtile[t] + (b_vtile[t - 1] if t > 0 else [])
            b0 = 1 if t == 0 else 0
            kb.dma(V1[:, b0:5, :], v_scr[(4 * t - 1 + b0) * 128:(t + 1) * T, hh, :].rearrange("(b p) c -> p b c", p=128),
                   rd2, b_V1)
            if t > 0:
                kb.dma(V4[:, 0:4, :], v_scr[(t - 1) * T:t * T, hh, :].rearrange("(p r) c -> p r c", r=4),
                       b_vtile[t - 1], [b_V4[0]])
            kb.dma(V4[:, 4:8, :], v_scr[t * T:(t + 1) * T, hh, :].rearrange("(p r) c -> p r c", r=4),
                   b_vtile[t], [b_V4[1]])
            rd_v = [b for tt_ in range(max(0, t - 3), t + 1) for b in b_vtile[tt_]]
            kb.dma(V16A[0:nk, :, :], v_scr[16 * ks:16 * (ks + nk), hh, :].rearrange("(p r) c -> p r c", r=16),
                   rd_v, [b_V16A])
            if t >= 4:
                kb.dma(V16O[0:32, :, :], v_scr[T * (t - 4):T * (t - 3), hh, :].rearrange("(p r) c -> p r c", r=16),
                       b_vtile[t - 4], [b_V16O])
            tasks = []
            for hl in range(4):
                tasks += head_tasks(hl, hh * 4 + hl)
            LOOK = 2
            for i in range(min(LOOK, len(tasks))):
                tasks[i][0]()
            DEFER = 3
            for i in range(len(tasks) + DEFER):
                if i + LOOK < len(tasks):
                    tasks[i + LOOK][0]()
                if i < len(tasks):
                    tasks[i][1]()
                if 0 <= i - DEFER < len(tasks) and len(tasks[i - DEFER]) > 2:
                    tasks[i - DEFER][2]()
        for g in range(4):
            bank, b_bank = psum("A")
            for tc in range(4):
                kb.mm(bank[:, tc * 128:(tc + 1) * 128], vn[:, tc, g * 128:(g + 1) * 128], wsb[:, g, :], True, True,
                      [b_vn[tc], b_wsb], [b_bank])
            f, b_f = tmpf()
            for tc in range(4):
                kb.tt(f[:, tc * 128:(tc + 1) * 128], bank[:, tc * 128:(tc + 1) * 128],
                      rows[:, 512 + g * 128: 512 + (g + 1) * 128], ALU.add, [b_bank, b_rows], [b_f])
            kb.tt(yT[:, g, :], f[:, :], ua[:, g, :], ALU.mult, [b_f, b_ua[g]], [b_yT[g]])
        wo = w_kpn(w_out)
        for half in range(2):
            sl, b_sl = get_w(f"wout{half}", [(8, 512, 0, 512, wo[:, :, half * 512:(half + 1) * 512], None)])
            for fcl in range(4):
                fc = half * 4 + fcl
                bank, b_bank = proj_fm(sl, b_sl, 512, fcl * 128, yT, b_yT)
                kb.stt(xt[:, fc, :], bank[:, :], gatecol(0, 0, fc), xt[:, fc, :], ALU.mult, ALU.add,
                       [b_bank, b_modc[0], b_xt[fc]], [b_xt[fc]])

    b_vtile = {}

    def ffn(l, t):
        arena_barrier()
        norm_mod(l, 1)
        wk = w_kpn(up_w[l])
        wname = f"up{l}"
        pend = []
        sas = {}
        for pr in range(11):
            sl, b_sl = get_w(f"{wname}_{pr}", [(8, 512, 0, 256, wk[:, :, pr * 256:(pr + 1) * 256], 0),
                                               (8, 512, 256, 256, wk[:, :, FF + pr * 256: FF + (pr + 1) * 256], 1)])
            chans = [2 * pr, 2 * pr + 1, 22 + 2 * pr, 22 + 2 * pr + 1]
            for ci, ch in enumerate(chans):
                bank, b_bank = proj_fm(sl, [b_sl[ci // 2]], 512, ci * 128, hT, b_hT)
                zi = kb.rr("zs", 3)
                z, b_z = zs[zi], b_zs[zi]
                ai = kb.rr("acc", 3)
                a, b_a = acc[ai], b_acc[ai]
                hcol = halo[:, (l * NCH + ch) * 2:(l * NCH + ch) * 2 + 2]
                kb.act(z[:, 2:T + 2], bank[:, :], AF.Identity, [b_bank], [b_z])
                kb.act(z[:, 0:2], hcol, AF.Identity, [b_halo[l][ch]], [b_zh[zi]])
                wo_ = COLS["fdw"] + (l * NCH + ch) * 3
                kb.act(a[:, :], bank[:, :], AF.Identity, [b_bank, b_cols], [b_a], scale=cols[:, wo_ + 2:wo_ + 3],
                       bias=col("fdb", l * NCH + ch))
                kb.act(hcol, bank[:, T - 2:T], AF.Identity, [b_bank], [b_halo[l][ch]])
                kb.stt(a[:, :], z[:, 1:T + 1], cols[:, wo_ + 1:wo_ + 2], a[:, :], ALU.mult, ALU.add,
                       [b_z, b_zh[zi], b_a, b_cols], [b_a])
                kb.stt(a[:, :], z[:, 0:T], cols[:, wo_:wo_ + 1], a[:, :], ALU.mult, ALU.add,
                       [b_z, b_zh[zi], b_a, b_cols], [b_a])
                if pend:
                    pend.pop(0)()

                def tail(ci=ci, a=a, b_a=b_a, pr=pr):
                    if ci < 2:
                        si = kb.rr("sa", 2)
                        kb.act(sa[si][:, :], a[:, :], AF.Silu, [b_a], [b_sa[si]])
                        sas[ci] = si
                    else:
                        si = sas[ci - 2]
                        k_ = 2 * pr + (ci - 2)
                        kb.tt(gT[:, k_, :], sa[si][:, :], a[:, :], ALU.mult, [b_sa[si], b_a], [b_gT[k_]])

                pend.append(tail)
        while pend:
            pend.pop(0)()
        wd = down_w[l].rearrange("(k p) n -> p k n", p=128)
        dname = f"down{l}"
        for cp in range(4):
            banks = [psum("A") for _ in range(2)]
            for kh in range(2):
                sl, b_sl = get_w(f"{dname}_{cp}_{kh}", [(11, 256, 0, 256, wd[:, kh * 11:(kh + 1) * 11, cp * 256:(cp + 1) * 256], None)])
                for fl in range(2):
                    bank, b_bank = banks[fl]
                    for kk in range(11):
                        k_ = kh * 11 + kk
                        kb.mm(bank[:, :], sl[:, kk * 256 + fl * 128: kk * 256 + (fl + 1) * 128], gT[:, k_, :],
                              k_ == 0, k_ == 21, b_sl + [b_gT[k_]], [b_bank])
            for fl in range(2):
                fc = cp * 2 + fl
                bank, b_bank = banks[fl]
                kb.stt(xt[:, fc, :], bank[:, :], gatecol(l, 1, fc), xt[:, fc, :], ALU.mult, ALU.add,
                       [b_bank, b_modc[l], b_xt[fc]], [b_xt[fc]])

    def l1_mixer(t):
        arena_barrier()
        norm_mod(1, 0)
        wk = w_kpn(pw1)
        for c in range(8):
            if t == 0:
                kb.memset(yglu[:, c, 0:30], 0.0, [b_yglu[c]])
            else:
                kb.copy(yglu[:, c, 0:30], yhalo[:, c, :], [b_yhalo[c]], [b_yglu[c]])
        for pr in range(4):
            sl, b_sl = get_w(f"pw1_{pr}", [(8, 512, 0, 256, wk[:, :, pr * 256:(pr + 1) * 256], 0),
                                           (8, 512, 256, 256, wk[:, :, D + pr * 256: D + (pr + 1) * 256], 1)])
            sgs = []
            for ci in (2, 3, 0, 1):
                bank, b_bank = proj_fm(sl, [b_sl[ci // 2]], 512, ci * 128, hT, b_hT)
                if ci >= 2:
                    ch = 8 + 2 * pr + (ci - 2)
                    sg, b_sg = tmpf()
                    kb.act(sg[:, :], bank[:, :], AF.Sigmoid, [b_bank, b_cols], [b_sg], bias=col("pw1b", ch))
                    sgs.append((sg, b_sg))
                else:
                    ch = 2 * pr + ci
                    sg, b_sg = sgs[ci]
                    kb.stt(yglu[:, ch, 30:30 + T], bank[:, :], col("pw1b", ch), sg[:, :], ALU.add, ALU.mult,
                           [b_bank, b_sg, b_cols], [b_yglu[ch]])
        mean, b_mean = psum("C")
        ex2, b_ex2 = psum("D")
        for c in range(8):
            def build_dg(sl_, b_sl_, c=c):
                kb.tt(sl_[:, 0:31 * 128].rearrange("p (j m) -> p j m", j=31),
                      cb("ident", 128).unsqueeze(1).to_broadcast([128, 31, 128]),
                      col("cdw", c * 31, 31).unsqueeze(2).to_broadcast([128, 31, 128]), ALU.mult,
                      [b_cbf, b_cols], b_sl_)

            sl, b_sl = get_w(f"dg{c}", [], build=build_dg)
            bank, b_bank = psum("A")
            for j in range(31):
                kb.mm(bank[:, :], sl[:, j * 128:(j + 1) * 128], yglu[:, c, j:j + T], j == 0, j == 30,
                      b_sl + [b_yglu[c]], [b_bank])
            kb.copy(yhalo[:, c, :], yglu[:, c, T:T + 30], [b_yglu[c]], [b_yhalo[c]])
            kb.act(ycv[:, c, :], bank[:, :], AF.Identity, [b_bank, b_cols], [b_ycv[c]], bias=col("cdb", c))
            ycb, b_ycb = tmpb()
            kb.act(ycb[:, :], bank[:, :], AF.Identity, [b_bank, b_cols], [b_ycb], bias=col("cdb", c))
            sq, b_sq = tmpb()
            kb.act(sq[:, :], bank[:, :], AF.Square, [b_bank, b_cols], [b_sq], bias=col("cdb", c))
            kb.mm(mean[:, :], cb("ones1024", 128), ycb[:, :], c == 0, c == 7, [b_ycb, b_cbf], [b_mean])
            kb.mm(ex2[:, :], cb("ones1024", 128), sq[:, :], c == 0, c == 7, [b_sq, b_cbf], [b_ex2])
        mu, b_mu = ln_mu, b_ln_mu
        kb.act(mu[:, :], mean[:, :], AF.Identity, [b_mean], [b_mu])
        msq, b_msq = tmpf()
        kb.tt(msq[:, :], mu[:, :], mu[:, :], ALU.mult, [b_mu], [b_msq])
        var, b_var = rstd_t, b_rstd_t
        kb.tt(var[:, :], ex2[:, :], msq[:, :], ALU.subtract, [b_ex2, b_msq], [b_var])
        kb.act(var[:, :], var[:, :], AF.Ln, [b_var, b_cols], [b_var], bias=col("eps"))
        kb.act(var[:, :], var[:, :], AF.Exp, [b_var], [b_var], scale=-0.5)
        for c in range(8):
            d_, b_d = tmpf()
            kb.tt(d_[:, :], ycv[:, c, :], mu[:, :], ALU.subtract, [b_ycv[c], b_mu], [b_d])
            kb.tt(d_[:, :], d_[:, :], var[:, :], ALU.mult, [b_d, b_var], [b_d])
            kb.act(hT[:, c, :], d_[:, :], AF.Silu, [b_d, b_cols], [b_hT[c]], scale=col("lng", c), bias=col("lnb", c))
        wp = w_kpn(pw2)
        for half in range(2):
            sl, b_sl = get_w(f"pw2_{half}", [(8, 512, 0, 512, wp[:, :, half * 512:(half + 1) * 512], None)])
            for fcl in range(4):
                fc = half * 4 + fcl
                bank, b_bank = proj_fm(sl, b_sl, 512, fcl * 128, hT, b_hT)
                kb.stt(xt[:, fc, :], bank[:, :], gatecol(1, 0, fc), xt[:, fc, :], ALU.mult, ALU.add,
                       [b_bank, b_modc[1], b_xt[fc]], [b_xt[fc]])
                kb.ts(xt[:, fc, :], xt[:, fc, :], dcol[:, 32 + fc:33 + fc], None, ALU.add, None,
                      [b_xt[fc], b_dcol[1]], [b_xt[fc]])

    outs = []
    for t in range(nt):
        S.epoch = t + 1
        l0_mixer(t)
        dbg_dump(0, t)
        if dbg and t == 0:
            for c in range(8):
                kb.dma(dbgT[3, c, :, 0:T], hT[:, c, :], [b_hT[c]], [])
                kb.dma(dbgT[3, c, :, T:2 * T], yT[:, c, :], [b_yT[c]], [])
            for c in range(4):
                kb.dma(dbgT[3, c, :, 2 * T:3 * T], qT[:, c, :], [b_qT[c]], [])
                kb.dma(dbgT[3, c, :, 3 * T:4 * T], kT[:, c, 0:T], [b_kT[c][0]], [])
            kb.dma(dbgT[3, 4, :, 2 * T:3 * T], cst[:, 0, :], [b_cst], [])
            kb.dma(dbgT[3, 5, :, 2 * T:3 * T], cst[:, 1, :], [b_cst], [])
        ffn(0, t)
        dbg_dump(1, t)
        l1_mixer(t)
        dbg_dump(2, t)
        ffn(1, t)
        for c in range(8):
            outs.append(kb.dma(outT[c, :, t * T:(t + 1) * T], xt[:, c, :], [b_xt[c]], []))
        if t + 1 < nt:
            load_x(t + 1)
    S.emit(nc, final_deps=outs)
    _CACHE["sbuf_left"] = nc.sbuf_bytes_remaining
    kb.es.close()
    return nc


_CACHE = {}


def _prep_shared(inp):
    f = np.float32
    cols = np.zeros((128, NCOL), f)

    def put(name, arr):
        cols[:, COLS[name]:COLS[name] + arr.shape[1]] = arr

    def pc(v):
        return np.ascontiguousarray(v.reshape(-1, 128).T)

    put("adab", np.concatenate([pc(inp["ada_b"][l]) for l in range(2)], 1))
    put("gmix", np.concatenate([pc(inp["norm_mix_g"][l]) for l in range(2)], 1))
    put("gffn", np.concatenate([pc(inp["norm_ffn_g"][l]) for l in range(2)], 1))
    put("gq", np.tile(inp["b_q_norm_g"][0], 2)[:, None])
    put("gk", np.tile(inp["b_k_norm_g"][0], 2)[:, None])
    put("pw1b", pc(inp["conv_pw1_b"][0]))
    cdw = inp["conv_dw_w"][0]
    put("cdw", np.ascontiguousarray(cdw.reshape(31, 8, 128).transpose(2, 1, 0)).reshape(128, 248))
    put("cdb", pc(inp["conv_dw_b"][0]))
    put("lng", pc(inp["conv_ln_g"][0]))
    put("lnb", pc(inp["conv_ln_b"][0]))
    put("pw2b", pc(inp["conv_pw2_b"][0]))
    fdw = inp["ffn_dw_w"]
    put("fdw", np.ascontiguousarray(fdw.reshape(2, 3, NCH, 128).transpose(3, 0, 2, 1)).reshape(128, 264))
    put("fdb", np.ascontiguousarray(inp["ffn_dw_b"].reshape(2, NCH, 128).transpose(2, 0, 1)).reshape(128, 88))
    invf = (1.0 / (10000.0 ** (np.arange(0, 64, 2, dtype=np.float32) / 64.0))).astype(f)
    put("invf", np.tile(invf, 4)[:, None])
    put("eps", np.full((128, 1), EPS, f))
    put("halfpi", np.full((128, 1), 0.5 * math.pi, f))
    rows = np.zeros((128, 1024), f)
    rows[:, 0:512] = inp["a_vnorm_g"][0].reshape(1, 512)
    rows[:, 512:1024] = inp["a_spatial_b"][0].reshape(1, 512)
    wsT = np.ascontiguousarray(inp["a_spatial_w"][0].transpose(2, 0, 1))
    p = np.arange(128)
    wmask = (p[None, :] >= p[:, None]).astype(f)
    return {
        "ada_w": np.ascontiguousarray(inp["ada_w"]), "w_in": np.ascontiguousarray(inp["ab_w_in"][0]),
        "w_out": np.ascontiguousarray(inp["ab_w_out"][0]), "up_w": np.ascontiguousarray(inp["ffn_up_w"]),
        "down_w": np.ascontiguousarray(inp["ffn_down_w"]), "pw1": np.ascontiguousarray(inp["conv_pw1_w"][0]),
        "pw2": np.ascontiguousarray(inp["conv_pw2_w"][0]), "wsT": wsT, "cols": cols, "rows": rows,
        "cbf": _const_bf16(), "wmask": wmask,
    }


def _in_maps(inp, n_cores=8):
    inp = {k: np.asarray(v) for k, v in inp.items()}
    shared = _prep_shared(inp)
    maps = []
    for b in range(n_cores):
        m = dict(shared)
        m["xT"] = np.ascontiguousarray(inp["x"][b].T).reshape(8, 128, SEQ)
        m["cT"] = np.ascontiguousarray(inp["c"][b].reshape(8, 128).T)
        m["posr"] = np.ascontiguousarray(np.broadcast_to(inp["positions"][b].astype(np.int32)[None, :], (128, SEQ)))
        maps.append(m)
    return maps


def kernel(**inputs):
    if "nc" not in _CACHE:
        _CACHE["nc"] = build_program()
    nc = _CACHE["nc"]
    maps = _in_maps(inputs)
    res = run_bass_kernel_spmd(nc, maps, core_ids=list(range(8)))
    out = np.stack([r["outT"].reshape(D, SEQ).T for r in res.results], 0)
    return np.ascontiguousarray(out.astype(np.float32))
```
